# Optimizing a Trainium2 kernel written in Bass

```python
import math
import jax, jax.numpy as jnp
from jax import lax
import numpy as np

D_MODEL = 2048
BATCH = 8
SEQ = 4096
DEPTH = 1
DEC_BATCH = 16
DEC_SEQ = 32
PAST_LEN = 4096

CHUNK = 64
N_PREV_CHUNKS = 8
BAND = (N_PREV_CHUNKS + 1) * CHUNK
HEAD_DIM = 128
N_HEADS_FOX = 8
N_HEADS_BAND = 8
D_FOX = N_HEADS_FOX * HEAD_DIM
D_BAND = N_HEADS_BAND * HEAD_DIM
D_FF = ((8 * D_MODEL + 3 * 256 - 1) // (3 * 256)) * 256
REL_CLIP = 128
Q_BLOCK = 128
FORGET_BIAS = 3.0
RMS_EPS = 1e-6
NEG_INF = -1e30
N_IN = 3 * D_FOX + N_HEADS_FOX + 3 * D_BAND + 2 * D_MODEL

kernel_name = "fox_chunkband_hybrid_stream_step"


def _rmsnorm(x, g):
    xf = x.astype(jnp.float32)
    y = xf * lax.rsqrt(jnp.mean(xf * xf, axis=-1, keepdims=True) + RMS_EPS)
    return (y * g.astype(jnp.float32)).astype(x.dtype)


def _mod_norm(x, g, shift, scale):
    return _rmsnorm(x, g) * (1.0 + scale[:, None, :]) + shift[:, None, :]


def _ada(c, w_ada, b_ada):
    m = jax.nn.silu(c) @ w_ada + b_ada
    return jnp.split(m, 6, axis=-1)


def _mixer_in(x, shift, scale, g, w_in, b_f):
    B, S, _ = x.shape
    h = _mod_norm(x, g, shift, scale)
    z = h @ w_in
    sizes = [D_FOX, D_FOX, D_FOX, N_HEADS_FOX, D_BAND, D_BAND, D_BAND, D_MODEL]
    cuts = [int(v) for v in np.cumsum(sizes)]
    qa, ka, va, fa, qb, kb, vb, za, zb = jnp.split(z, cuts, axis=-1)
    logf = jax.nn.log_sigmoid((fa + b_f).astype(jnp.float32))
    ha = lambda t: t.reshape(B, S, N_HEADS_FOX, HEAD_DIM)
    hb = lambda t: t.reshape(B, S, N_HEADS_BAND, HEAD_DIM)
    return ha(qa), ha(ka), ha(va), logf, hb(qb), hb(kb), hb(vb), za, zb


def _fox_prompt(q, k, v, logf):
    B, S, H, Dh = q.shape
    nb = S // Q_BLOCK
    Ft = jnp.cumsum(logf, axis=1).transpose(0, 2, 1)
    qb = q.reshape(B, nb, Q_BLOCK, H, Dh).transpose(1, 0, 2, 3, 4)
    Fq = Ft.reshape(B, H, nb, Q_BLOCK).transpose(2, 0, 1, 3)
    kpos = jnp.arange(S)
    inv = 1.0 / math.sqrt(Dh)

    def one_block(args):
        i, qi, fi = args
        s = jnp.einsum('bqhd,bkhd->bhqk', qi, k, preferred_element_type=jnp.float32) * inv
        s = s + fi[..., None] - Ft[:, :, None, :]
        qpos = i * Q_BLOCK + jnp.arange(Q_BLOCK)
        s = jnp.where(kpos[None, :] <= qpos[:, None], s, NEG_INF)
        p = jax.nn.softmax(s, axis=-1)
        return jnp.einsum('bhqk,bkhd->bqhd', p.astype(v.dtype), v)

    o = lax.map(one_block, (jnp.arange(nb), qb, Fq))
    return o.transpose(1, 0, 2, 3, 4).reshape(B, S, H * Dh)


def _fox_sample(q, k, v, logf, ck, cv, clogf):
    B, T, H, Dh = q.shape
    P = ck.shape[1]
    k_all = jnp.concatenate([ck, k], axis=1)
    v_all = jnp.concatenate([cv, v], axis=1)
    Ft = jnp.cumsum(jnp.concatenate([clogf.astype(jnp.float32), logf], axis=1), axis=1).transpose(0, 2, 1)
    s = jnp.einsum('bqhd,bkhd->bhqk', q, k_all, preferred_element_type=jnp.float32) / math.sqrt(Dh)
    s = s + Ft[:, :, P:, None] - Ft[:, :, None, :]
    mask = jnp.arange(P + T)[None, :] <= (P + jnp.arange(T))[:, None]
    s = jnp.where(mask, s, NEG_INF)
    p = jax.nn.softmax(s, axis=-1)
    return jnp.einsum('bhqk,bkhd->bqhd', p.astype(v_all.dtype), v_all).reshape(B, T, H * Dh)


def _rel_bias(table, rel):
    return table[jnp.clip(rel, -REL_CLIP, REL_CLIP) + REL_CLIP].transpose(2, 0, 1).astype(jnp.float32)


def _band_prompt(q, k, v, table):
    B, S, H, Dh = q.shape
    nc = S // CHUNK
    pad = N_PREV_CHUNKS * CHUNK
    kp = jnp.pad(k, ((0, 0), (pad, 0), (0, 0), (0, 0)))
    vp = jnp.pad(v, ((0, 0), (pad, 0), (0, 0), (0, 0)))
    qc = q.reshape(B, nc, CHUNK, H, Dh).transpose(1, 0, 2, 3, 4)
    rel = pad + jnp.arange(CHUNK)[:, None] - jnp.arange(BAND)[None, :]
    bias = _rel_bias(table, rel)
    inv = 1.0 / math.sqrt(Dh)

    def one_chunk(args):
        n, qn = args
        kn = lax.dynamic_slice_in_dim(kp, n * CHUNK, BAND, axis=1)
        vn = lax.dynamic_slice_in_dim(vp, n * CHUNK, BAND, axis=1)
        s = jnp.einsum('bqhd,bkhd->bhqk', qn, kn, preferred_element_type=jnp.float32) * inv + bias
        valid = (n - N_PREV_CHUNKS) * CHUNK + jnp.arange(BAND) >= 0
        s = jnp.where(valid[None, None, None, :], s, NEG_INF)
        p = jax.nn.softmax(s, axis=-1)
        return jnp.einsum('bhqk,bkhd->bqhd', p.astype(vn.dtype), vn)

    o = lax.map(one_chunk, (jnp.arange(nc), qc))
    return o.transpose(1, 0, 2, 3, 4).reshape(B, S, H * Dh)


def _band_sample(q, k, v, ck, cv, table):
    B, T, H, Dh = q.shape
    lb = ck.shape[1]
    k_all = jnp.concatenate([ck, k], axis=1)
    v_all = jnp.concatenate([cv, v], axis=1)
    rel = lb + jnp.arange(T)[:, None] - jnp.arange(lb + T)[None, :]
    s = jnp.einsum('bqhd,bkhd->bhqk', q, k_all, preferred_element_type=jnp.float32) / math.sqrt(Dh)
    s = s + _rel_bias(table, rel)
    p = jax.nn.softmax(s, axis=-1)
    return jnp.einsum('bhqk,bkhd->bqhd', p.astype(v_all.dtype), v_all).reshape(B, T, H * Dh)


def _merge(oa, ob, za, zb, w_oa, w_ob, w_out):
    m = jax.nn.sigmoid(za) * (oa @ w_oa) + jax.nn.sigmoid(zb) * (ob @ w_ob)
    return m @ w_out


def _swiglu(h, w_gate, w_up, w_down):
    return (jax.nn.silu(h @ w_gate) * (h @ w_up)) @ w_down


def _layer(x, c, attend, w_ada, b_ada, g_mix, w_in, b_f, w_oa, w_ob, w_out, g_ffn, w_gate, w_up, w_down):
    sh1, sc1, gt1, sh2, sc2, gt2 = _ada(c, w_ada, b_ada)
    qa, ka, va, lf, qb, kb, vb, za, zb = _mixer_in(x, sh1, sc1, g_mix, w_in, b_f)
    oa, ob = attend(qa, ka, va, lf, qb, kb, vb)
    x = x + gt1[:, None, :] * _merge(oa, ob, za, zb, w_oa, w_ob, w_out)
    x = x + gt2[:, None, :] * _swiglu(_mod_norm(x, g_ffn, sh2, sc2), w_gate, w_up, w_down)
    return x, ka, va, lf, kb, vb


def setup_inputs(seed: int = 0) -> dict:
    key = jax.random.key(seed)
    ks = jax.random.split(key, 24)
    f32 = jnp.float32
    nrm = lambda k, shape, s: s * jax.random.normal(k, shape, f32)
    lb = min(N_PREV_CHUNKS * CHUNK, PAST_LEN)
    return {
        "x_prompt": nrm(ks[0], (BATCH, SEQ, D_MODEL), 1.0),
        "x_sample": nrm(ks[1], (DEC_BATCH, DEC_SEQ, D_MODEL), 1.0),
        "cache_fox_k": nrm(ks[2], (DEPTH, DEC_BATCH, PAST_LEN, N_HEADS_FOX, HEAD_DIM), 1.0),
        "cache_fox_v": nrm(ks[3], (DEPTH, DEC_BATCH, PAST_LEN, N_HEADS_FOX, HEAD_DIM), 1.0),
        "cache_fox_logf": jax.nn.log_sigmoid(FORGET_BIAS + nrm(ks[4], (DEPTH, DEC_BATCH, PAST_LEN, N_HEADS_FOX), 1.0)),
        "cache_band_k": nrm(ks[5], (DEPTH, DEC_BATCH, lb, N_HEADS_BAND, HEAD_DIM), 1.0),
        "cache_band_v": nrm(ks[6], (DEPTH, DEC_BATCH, lb, N_HEADS_BAND, HEAD_DIM), 1.0),
        "c_prompt": nrm(ks[7], (BATCH, D_MODEL), 1.0),
        "c_sample": nrm(ks[8], (DEC_BATCH, D_MODEL), 1.0),
        "w_ada": nrm(ks[9], (DEPTH, D_MODEL, 6 * D_MODEL), 0.5 * D_MODEL ** -0.5),
        "b_ada": nrm(ks[10], (DEPTH, 6 * D_MODEL), 0.01),
        "g_mix": 1.0 + nrm(ks[11], (DEPTH, D_MODEL), 0.05),
        "w_in": nrm(ks[12], (DEPTH, D_MODEL, N_IN), D_MODEL ** -0.5),
        "b_f": FORGET_BIAS + nrm(ks[13], (DEPTH, N_HEADS_FOX), 0.1),
        "rel_bias": nrm(ks[14], (DEPTH, 2 * REL_CLIP + 1, N_HEADS_BAND), 0.1),
        "w_oa": nrm(ks[15], (DEPTH, D_FOX, D_MODEL), D_FOX ** -0.5),
        "w_ob": nrm(ks[16], (DEPTH, D_BAND, D_MODEL), D_BAND ** -0.5),
        "w_out": nrm(ks[17], (DEPTH, D_MODEL, D_MODEL), D_MODEL ** -0.5),
        "g_ffn": 1.0 + nrm(ks[18], (DEPTH, D_MODEL), 0.05),
        "w_gate": nrm(ks[19], (DEPTH, D_MODEL, D_FF), D_MODEL ** -0.5),
        "w_up": nrm(ks[20], (DEPTH, D_MODEL, D_FF), D_MODEL ** -0.5),
        "w_down": nrm(ks[21], (DEPTH, D_FF, D_MODEL), D_FF ** -0.5),
        "g_final": 1.0 + nrm(ks[22], (D_MODEL,), 0.05),
    }


def reference(x_prompt, x_sample, cache_fox_k, cache_fox_v, cache_fox_logf, cache_band_k, cache_band_v,
              c_prompt, c_sample, w_ada, b_ada, g_mix, w_in, b_f, rel_bias, w_oa, w_ob, w_out,
              g_ffn, w_gate, w_up, w_down, g_final):
    xp, xs = x_prompt, x_sample
    fk_p, fv_p, fl_p, bk_p, bv_p = [], [], [], [], []
    fk_s, fv_s, fl_s, bk_s, bv_s = [], [], [], [], []
    for l in range(DEPTH):
        params = (w_ada[l], b_ada[l], g_mix[l], w_in[l], b_f[l], w_oa[l], w_ob[l], w_out[l],
                  g_ffn[l], w_gate[l], w_up[l], w_down[l])
        tbl = rel_bias[l]
        attend_p = lambda qa, ka, va, lf, qb, kb, vb: (
            _fox_prompt(qa, ka, va, lf), _band_prompt(qb, kb, vb, tbl))
        xp, ka, va, lf, kb, vb = _layer(xp, c_prompt, attend_p, *params)
        lbp = min(N_PREV_CHUNKS * CHUNK, kb.shape[1])
        fk_p.append(ka); fv_p.append(va); fl_p.append(lf)
        bk_p.append(kb[:, -lbp:]); bv_p.append(vb[:, -lbp:])

        ck, cv, cl, cbk, cbv = cache_fox_k[l], cache_fox_v[l], cache_fox_logf[l], cache_band_k[l], cache_band_v[l]
        attend_s = lambda qa, ka, va, lf, qb, kb, vb: (
            _fox_sample(qa, ka, va, lf, ck, cv, cl), _band_sample(qb, kb, vb, cbk, cbv, tbl))
        xs, ka, va, lf, kb, vb = _layer(xs, c_sample, attend_s, *params)
        fk_s.append(ka); fv_s.append(va); fl_s.append(lf)
        bk_s.append(kb); bv_s.append(vb)

    y_prompt = _rmsnorm(xp, g_final)
    y_sample = _rmsnorm(xs, g_final)
    return (y_prompt, y_sample,
            jnp.stack(fk_p), jnp.stack(fv_p), jnp.stack(fl_p), jnp.stack(bk_p), jnp.stack(bv_p),
            jnp.stack(fk_s), jnp.stack(fv_s), jnp.stack(fl_s), jnp.stack(bk_s), jnp.stack(bv_s))
```

```python
import numpy as np
import concourse.bass as bass
import concourse.mybir as mybir
from concourse.bass_utils import run_bass_kernel_spmd

F32 = mybir.dt.float32
BF16 = mybir.dt.bfloat16
AF = mybir.ActivationFunctionType
ALU = mybir.AluOpType

D = 2048
NH = 8
HD = 128
DFF = 5632
SEQ = 4096
PAST = 4096
LB = 512
TS = 32
NCORES = 8
KC = D // 128
FC = DFF // 128
INV = 1.0 / float(np.sqrt(HD))
NEGB = -30000.0
EPS = 1e-6
STQ = "pool"
WMODE = 1


class _Op:
    __slots__ = ("eng", "fn", "deps", "seq", "sig", "dsem", "dval", "idx")


class Prog:
    ENGS = ("pe", "act", "dve", "pool", "sp")
    SAME_ENG_SYNC = {"pe": False, "act": True, "dve": True, "pool": True, "sp": False}

    def __init__(self, nc):
        self.nc = nc
        self.ops = {e: [] for e in self.ENGS}
        self.last_w = {}
        self.readers = {}
        self.dma_cnt = {}
        self.nops = 0

    def add(self, eng, fn, reads=(), writes=(), dsem=None):
        op = _Op()
        op.eng = eng
        op.fn = fn
        op.sig = False
        op.seq = None
        op.dsem = dsem
        op.idx = self.nops
        self.nops += 1
        deps = {}

        def dep_on(d):
            if d is None:
                return
            if d.dsem is not None:
                deps[("d", id(d.dsem))] = (d.dsem, self.dma_cnt[id(d.dsem)])
            else:
                if d.eng == eng and not self.SAME_ENG_SYNC[eng]:
                    return
                d.sig = True
                key = ("e", d.eng)
                if key not in deps or deps[key].idx < d.idx:
                    deps[key] = d
        for k in reads:
            dep_on(self.last_w.get(k))
            if k.startswith("PS") or k.startswith("PB"):
                for r in self.readers.get(k, ()):
                    if r.eng != eng:
                        dep_on(r)
        for k in writes:
            dep_on(self.last_w.get(k))
            for r in self.readers.get(k, ()):
                dep_on(r)
        for k in reads:
            self.readers.setdefault(k, []).append(op)
        for k in writes:
            self.last_w[k] = op
            self.readers[k] = []
        if dsem is not None:
            self.dma_cnt[id(dsem)] = self.dma_cnt.get(id(dsem), 0) + 16
            op.dval = self.dma_cnt[id(dsem)]
        op.deps = list(deps.values())
        self.ops[eng].append(op)
        return op

    def emit(self):
        nc = self.nc
        esem = {e: nc.alloc_semaphore("es_" + e) for e in self.ENGS}
        for e in self.ENGS:
            c = 0
            for op in self.ops[e]:
                if op.dsem is None and op.sig:
                    c += 1
                    op.seq = c
        ops = self.ops
        all_dsems = {}
        for e in self.ENGS:
            for op in ops[e]:
                if op.dsem is not None:
                    all_dsems[id(op.dsem)] = op.dsem
        dma_cnt = self.dma_cnt

        def run(e, eng):
            waited = {}
            for op in ops[e]:
                for d in op.deps:
                    if isinstance(d, tuple):
                        sem, val = d
                    else:
                        sem, val = esem[d.eng], d.seq
                    if waited.get(id(sem), 0) >= val:
                        continue
                    waited[id(sem)] = val
                    eng.wait_ge(sem, val)
                ins = op.fn(eng)
                if op.dsem is not None:
                    ins.then_inc(op.dsem, 16)
                elif op.sig:
                    ins.then_inc(esem[e], 1)
            if e == "sp":
                for sid, sem in all_dsems.items():
                    if waited.get(sid, 0) < dma_cnt[sid]:
                        eng.wait_ge(sem, dma_cnt[sid])

        with nc.Block() as block:
            @block.tensor
            def _(eng):
                run("pe", eng)

            @block.scalar
            def _(eng):
                run("act", eng)

            @block.vector
            def _(eng):
                run("dve", eng)

            @block.gpsimd
            def _(eng):
                run("pool", eng)

            @block.sync
            def _(eng):
                run("sp", eng)


WSPEC = {
    "qa": ("w_in", 0, D, 1024, "S"),
    "ka": ("w_in", 1024, D, 1024, "A"),
    "va": ("w_in", 2048, D, 1024, "A"),
    "qb": ("w_in", 3080, D, 1024, "S"),
    "kb": ("w_in", 4104, D, 1024, "A"),
    "vb": ("w_in", 5128, D, 1024, "A"),
    "za": ("w_in", 6152, D, 2048, "S"),
    "zb": ("w_in", 8200, D, 2048, "S"),
    "oa": ("w_oa", 0, 1024, 2048, "S8"),
    "ob": ("w_ob", 0, 1024, 2048, "S8"),
    "out": ("w_out", 0, D, 2048, "A"),
    "gate": ("w_gate", 0, D, DFF, "S"),
    "up": ("w_up", 0, D, DFF, "S"),
    "down": ("w_down", 0, DFF, 2048, "A"),
}
WORDER = ["ka", "va", "kb", "vb", "qa", "qb", "za", "oa", "zb", "ob", "out", "gate", "up", "down"]


def _ntiles(name):
    _, _, K, N, t = WSPEC[name]
    if t == "S":
        return N // 256
    if t == "S8":
        return N // 512
    nkh = (K // 128 + 7) // 8
    return (N // 512) * nkh


def build_program(ntiles=SEQ // 512, do_sample=True):
    nc = bass.Bass("TRN2", target_bir_lowering=False)
    P = Prog(nc)

    def din(name, shape, dt=F32):
        return nc.dram_tensor(name, list(shape), dt, kind="ExternalInput").ap()

    def dout(name, shape, dt=F32):
        return nc.dram_tensor(name, list(shape), dt, kind="ExternalOutput").ap()

    def dscr(name, shape, dt):
        return nc.dram_tensor(name, list(shape), dt, kind="Internal").ap()

    xp = din("xp", [SEQ, D])
    xs = din("xs", [2 * TS, D])
    cfk = din("cfk", [2, PAST, NH * HD])
    cfv = din("cfv", [2, PAST, NH * HD])
    cfl = din("cfl", [2, PAST, NH])
    cbk = din("cbk", [2, LB, NH * HD])
    cbv = din("cbv", [2, LB, NH * HD])
    c3 = din("c3", [3, D])
    w_ada = din("w_ada", [D, 6 * D])
    b_ada = din("b_ada", [96, 128])
    g_mix = din("g_mix", [16, 128])
    g_ffn = din("g_ffn", [16, 128])
    g_fin = din("g_fin", [1, D])
    b_f = din("b_f", [1, NH])
    relb = din("relb", [257, NH])
    wsrc = {
        "w_in": din("w_in", [D, 10248]),
        "w_oa": din("w_oa", [1024, D]),
        "w_ob": din("w_ob", [1024, D]),
        "w_out": din("w_out", [D, D]),
        "w_gate": din("w_gate", [D, DFF]),
        "w_up": din("w_up", [D, DFF]),
        "w_down": din("w_down", [DFF, D]),
    }
    yp = dout("yp", [SEQ, D])
    ys = dout("ys", [2 * TS, D])
    fkp = dout("fkp", [SEQ, 1024])
    fvp = dout("fvp", [SEQ, 1024])
    flp = dout("flp", [SEQ, NH])
    bkp = dout("bkp", [LB, 1024])
    bvp = dout("bvp", [LB, 1024])
    fks = dout("fks", [2 * TS, 1024])
    fvs = dout("fvs", [2 * TS, 1024])
    fls = dout("fls", [2 * TS, NH])
    bks = dout("bks", [2 * TS, 1024])
    bvs = dout("bvs", [2 * TS, 1024])

    wscr = {n: dscr("ws_" + n, [_ntiles(n), 128, 4096], BF16) for n in WSPEC}
    wf_scr = dscr("ws_f", [128, 16 * 8], BF16)
    kaT_scr = dscr("kaT_scr", [NH, 128, SEQ], BF16)
    va_scr = dscr("va_scr", [SEQ, 1024], BF16)
    kbT_scr = dscr("kbT_scr", [NH, 128, SEQ], BF16)
    vb_scr = dscr("vb_scr", [SEQ, 1024], BF16)
    adascr = dscr("adascr", [3, 96, 128], F32)
    extscr = dscr("extscr", [NH, 768], F32)

    off = [0]
    base0 = None

    def sb(name, shape, dt):
        return nc.alloc_sbuf_tensor(name, list(shape), dt)

    X = sb("X", [128, 4, D], F32)
    XN = sb("XN", [128, D], F32)
    hT = sb("hT", [128, KC, 512], BF16)
    NSLOT = 3
    WS = [sb("WS%d" % i, [128, 4096], BF16) for i in range(NSLOT)]
    GREP = sb("GREP", [128, 2, D], F32)
    GFIN = sb("GFIN", [128, D], F32)
    BH = sb("BH", [128, NH, 640], BF16)
    MODF = sb("MODF", [128, 3, 4, 16], F32)
    ADAT = sb("ADAT", [128, 96, 3], F32)
    identf = sb("identf", [128, 128], F32)
    identb = sb("identb", [128, 128], BF16)
    Ub = sb("Ub", [128, 128], BF16)
    Ltri = sb("Ltri", [128, 128], F32)
    Elast = sb("Elast", [128, 128], F32)
    Jf = sb("Jf", [128, 128], F32)
    E31 = sb("E31", [128, 128], F32)
    CL = sb("CL", [128, 32, NH], F32)
    onesb = sb("onesb", [128, 128], BF16)
    Fall = sb("Fall", [128, 34, NH], F32)
    Fref = sb("Fref", [128, NH], F32)
    NB = sb("NB", [128, 34, NH], F32)
    WF = sb("WF", [128, 16, 8], BF16)
    BFR = sb("BFR", [128, NH], F32)
    SSQ = sb("SSQ", [128, 4], F32)
    RSTD = sb("RSTD", [128, 4], F32)
    LFT = sb("LFT", [128, 4, NH], F32)
    LTMP = sb("LTMP", [128, NH], F32)
    SMALL = sb("SMALL", [128, 128], F32)
    DUMMY = sb("DUMMY", [128, 4], F32)
    SQJ = sb("SQJ", [128, D], BF16)
    PT = [sb("PT%d" % i, [128, 512], BF16) for i in range(3)]
    STGF = [sb("STGF%d" % i, [128, 512], F32) for i in range(2)]
    STGB = [sb("STGB%d" % i, [128, 512], BF16) for i in range(4)]
    KTB = [sb("KTB%d" % i, [128, 4, 128], BF16) for i in range(2)]
    RDEN = sb("RDEN", [128, 512], F32)
    SGS = [sb("SG%d" % i, [128, 512], F32) for i in range(4)]
    SG = SGS[0]
    MAS = [sb("MA%d" % i, [128, 512], F32) for i in range(4)]
    MTMP = sb("MTMP", [128, 512], F32)
    MTMP2 = sb("MTMP2", [128, 512], F32)
    ARENA = sb("ARENA", [128, FC * 512], BF16)
    abase = None

    def aview(name, elem_off, shape):
        n = 1
        for s in shape:
            n *= s
        flat = ARENA[:, elem_off:elem_off + n]
        if len(shape) == 1:
            return flat
        if len(shape) == 2:
            return flat.rearrange("p (a b) -> p a b", a=shape[0])
        raise ValueError

    actT = aview("actT", 0, [FC, 512])
    mT = aview("mT", 0, [KC, 512])
    QaT = aview("QaT", 0, [NH, 512])
    QbT = aview("QbT", 4096, [NH, 512])
    oaT = aview("oaT", 8192, [NH, 512])
    obT = aview("obT", 12288, [NH, 512])
    KS = [aview("KS%d" % i, 16384 + i * 1024, [1024]) for i in range(3)]
    VS = [aview("VS%d" % i, 19456 + i * 1024, [8, 128]) for i in range(3)]
    ALIAS = {
        "mT": ["QaT", "QbT"],
        "actT": ["mT", "QaT", "QbT", "oaT", "obT", "KS0", "KS1", "KS2", "VS0", "VS1", "VS2"],
    }

    def K_(name):
        return [name] + ALIAS.get(name, [])

    PS = [nc.alloc_psum_tensor("PS%d" % i, [128, 512], F32) for i in range(7)]
    PB = nc.alloc_psum_tensor("PB", [128, 1024], BF16)

    sems = {}

    def S_(name):
        if name not in sems:
            sems[name] = nc.alloc_semaphore("d_" + name)
        return sems[name]

    def dma(q, out, in_, reads, writes, sem, **kw):
        P.add(q, (lambda o, i, kw: lambda e: e.dma_start(out=o, in_=i, **kw))(out, in_, kw),
              reads=reads, writes=writes, dsem=S_(sem))

    def mm(out, lhsT, rhs, start, stop, reads, writes):
        P.add("pe", (lambda o, l, r, s0, s1: lambda e: e.matmul(o, lhsT=l, rhs=r, start=s0, stop=s1))(out, lhsT, rhs, start, stop),
              reads=reads, writes=writes)

    def tr(out, in_, ident, reads, writes):
        P.add("pe", (lambda o, i, d: lambda e: e.transpose(o, i, d))(out, in_, ident), reads=reads, writes=writes)

    def act(out, in_, func, reads, writes, bias=None, scale=None, accum=None):
        kw = {}
        if bias is not None:
            kw["bias"] = bias
        if scale is not None:
            kw["scale"] = scale
        if accum is not None:
            kw["accum_out"] = accum
        P.add("act", (lambda o, i, f, kw: lambda e: e.activation(out=o, in_=i, func=f, **kw))(out, in_, func, kw),
              reads=reads, writes=writes)

    def tt(eng, out, in0, in1, op, reads, writes):
        P.add(eng, (lambda o, a, b, op: lambda e: e.tensor_tensor(o, a, b, op))(out, in0, in1, op), reads=reads, writes=writes)

    def ts(eng, out, in0, s1, s2, op0, op1, reads, writes):
        if op1 is None:
            P.add(eng, (lambda o, a, s1, op0: lambda e: e.tensor_scalar(o, a, s1, None, op0))(out, in0, s1, op0), reads=reads, writes=writes)
        else:
            P.add(eng, (lambda o, a, s1, s2, op0, op1: lambda e: e.tensor_scalar(o, a, s1, s2, op0, op1))(out, in0, s1, s2, op0, op1), reads=reads, writes=writes)

    def cp(eng, out, in_, reads, writes):
        if eng == "act":
            act(out, in_, AF.Copy, reads, writes)
        else:
            P.add(eng, (lambda o, i: lambda e: e.tensor_copy(o, i))(out, in_), reads=reads, writes=writes)

    def memset(eng, ap, val, writes):
        P.add(eng, (lambda a, v: lambda e: e.memset(a, v))(ap, val), writes=writes)

    def asel(out, in_, pattern, cmp_op, fill, base, cm, reads, writes):
        P.add("pool", (lambda o, i, p, c, f, b, m: lambda e: e.affine_select(o, i, pattern=p, compare_op=c, fill=f, base=b, channel_multiplier=m))(out, in_, pattern, cmp_op, fill, base, cm),
              reads=reads, writes=writes)

    psrot = [0]

    def next_ps(n=4):
        i = psrot[0] % n
        psrot[0] += 1
        return i

    memset("pool", identf[:], 1.0, ["identf"])
    asel(identf[:], identf[:], [[-1, 128]], ALU.is_equal, 0.0, 0, 1, ["identf"], ["identf"])
    cp("dve", identb[:], identf[:], ["identf"], ["identb"])
    memset("pool", Ltri[:], 1.0, ["Ltri"])
    asel(Ltri[:], Ltri[:], [[1, 128]], ALU.is_ge, 0.0, 0, -1, ["Ltri"], ["Ltri"])
    cp("dve", Ub[:], Ltri[:], ["Ltri"], ["Ub"])
    memset("pool", Elast[:], 1.0, ["Elast"])
    asel(Elast[:], Elast[:], [[0, 128]], ALU.is_equal, 0.0, -127, 1, ["Elast"], ["Elast"])
    memset("pool", Jf[:], 1.0, ["Jf"])
    asel(Jf[:], Jf[:], [[1, 128]], ALU.is_equal, 0.0, -127, 1, ["Jf"], ["Jf"])
    memset("pool", E31[:], 1.0, ["E31"])
    asel(E31[:], E31[:], [[0, 128]], ALU.is_equal, 0.0, -31, 1, ["E31"], ["E31"])
    memset("dve", onesb[:], 1.0, ["onesb"])
    memset("dve", SSQ[:], 0.0, ["SSQ"])

    dma("sp", GFIN[:], g_fin.partition_broadcast(128), [], ["GFIN"], "gfin")
    dma("sp", BFR[:], b_f.partition_broadcast(128), [], ["BFR"], "bfr")

    def load_featmajor(src_ap, n, dst_ap, key, sem):
        dma("sp", SMALL[0:n, :], src_ap, [], ["SMALL"], sem)
        tr(PS[6][:, 0:n], SMALL[0:n, :], identf[0:n, 0:n], ["SMALL", "identf"], ["PS6"])
        cp("dve", dst_ap, PS[6][:, 0:n], ["PS6"], [key])

    BADAT = sb("BADAT", [128, 96], F32)
    GMIXT = sb("GMIXT", [128, 16], F32)
    GFFNT = sb("GFFNT", [128, 16], F32)
    load_featmajor(b_ada, 96, BADAT[:], "BADAT", "small")
    load_featmajor(g_mix, 16, GMIXT[:], "GMIXT", "small")
    load_featmajor(g_ffn, 16, GFFNT[:], "GFFNT", "small")

    CS = XN[0:3, :]
    SCT = sb("SCT", [128, KC, 3], BF16)
    dma("sp", CS, c3, [], ["XN"], "cs")
    act(CS, CS, AF.Silu, ["XN"], ["XN"])
    for kc in range(KC):
        tr(PS[6][:, kc * 3:(kc + 1) * 3], XN[0:3, kc * 128:(kc + 1) * 128], identf[0:3, 0:3], ["XN", "identf"], ["PS6"])
    cp("dve", SCT[:], PS[6][:, 0:48].rearrange("p (k r) -> p k r", r=3), ["PS6"], ["SCT"])
    for cg in range(48):
        slot = cg % NSLOT
        wv = WS[slot][:].rearrange("p (k c) -> p k c", c=256)
        dma("pool", wv, w_ada[:, cg * 256:(cg + 1) * 256].rearrange("(k p) c -> p k c", p=128), [], ["WS%d" % slot], "ws%d" % slot)
        for oc in range(2):
            col = (cg * 2 + oc) * 3
            for kc in range(KC):
                mm(PS[5][:, col:col + 3], wv[:, kc, oc * 128:(oc + 1) * 128], SCT[:, kc, :], kc == 0, kc == KC - 1,
                   ["WS%d" % slot, "SCT"], ["PS5"])
    for r in range(3):
        tt("dve", ADAT[:, :, r], PS[5][:, 0:288].rearrange("p (v r) -> p v r", r=3)[:, :, r], BADAT[:], ALU.add, ["PS5", "BADAT"], ["ADAT"])
    for r in range(3):
        P.add("dve", (lambda r: lambda e: e.scalar_tensor_tensor(MODF[:, r, 0, :], ADAT[:, 16:32, r], 1.0, GMIXT[:], ALU.add, ALU.mult))(r),
              reads=["ADAT", "GMIXT"], writes=["MODF"])
        cp("dve", MODF[:, r, 1, :], ADAT[:, 0:16, r], ["ADAT"], ["MODF"])
        P.add("dve", (lambda r: lambda e: e.scalar_tensor_tensor(MODF[:, r, 2, :], ADAT[:, 64:80, r], 1.0, GFFNT[:], ALU.add, ALU.mult))(r),
              reads=["ADAT", "GFFNT"], writes=["MODF"])
        cp("dve", MODF[:, r, 3, :], ADAT[:, 48:64, r], ["ADAT"], ["MODF"])
    for r in range(3):
        tr(PS[6][0:96, 0:128], ADAT[:, :, r], identf[:], ["ADAT", "identf"], ["PS6"])
        cp("dve", SMALL[0:96, :], PS[6][0:96, 0:128], ["PS6"], ["SMALL"])
        dma("sp", adascr[r], SMALL[0:96, :], ["SMALL"], ["adascr"], "small")
    adaflat = adascr.rearrange("r v p -> r (v p)")

    def load_gates(row, nrows):
        dma("sp", GREP[0:nrows, 0, :], adaflat[row:row + 1, 2 * D:3 * D].partition_broadcast(nrows), ["adascr"], ["GREP"], "grep")
        dma("sp", GREP[0:nrows, 1, :], adaflat[row:row + 1, 5 * D:6 * D].partition_broadcast(nrows), ["adascr"], ["GREP"], "grep")

    T8 = XN[0:8, 0:768]
    dma("sp", T8[:, 0:256], relb[1:257, :].rearrange("w h -> h w"), [], ["XN"], "t8", allow_slow_non_contiguous=True)
    memset("dve", T8[:, 256:768], 0.0, ["XN"])
    ts("dve", T8[:, 256:768], T8[:, 256:768], T8[:, 255:256], 0.0, ALU.add, ALU.add, ["XN"], ["XN"])
    dma("sp", extscr, T8, ["XN"], ["extscr"], "t8")
    RT = XN[:, 1024:1664]
    for h in range(NH):
        src = bass.AP(tensor=extscr.tensor, offset=h * 768, ap=[[1, 128], [1, 640]])
        dma("sp", RT, src, ["extscr"], ["XN"], "rt")
        for half in range(2):
            mm(PS[6][:, 0:320], Jf[:], RT[:, half * 320:(half + 1) * 320], True, True, ["Jf", "XN"], ["PS6"])
            cp("dve", BH[:, h, half * 320:(half + 1) * 320], PS[6][:, 0:320], ["PS6"], ["BH"])
    memset("pool", BH[0:64, :, 576:640], NEGB, ["BH"])
    memset("pool", BH[64:128, :, 0:64], NEGB, ["BH"])

    def convert(name):
        srcname, c0, K, N, t = WSPEC[name]
        src = wsrc[srcname]
        scr = wscr[name]
        for kc in range(K // 128):
            rows = src[kc * 128:(kc + 1) * 128, c0:c0 + N]
            if t == "S":
                s = rows.rearrange("p (g c) -> p g c", c=256)
                d_ = scr[:, :, kc * 256:(kc + 1) * 256].rearrange("g p c -> p g c")
            elif t == "S8":
                s = rows.rearrange("p (g c) -> p g c", c=512)
                d_ = scr[:, :, kc * 512:(kc + 1) * 512].rearrange("g p c -> p g c")
            else:
                nkh = (K // 128 + 7) // 8
                s = rows.rearrange("p (g c) -> p g c", c=512)
                sv = scr.rearrange("(g h) p e -> g h p e", h=nkh)
                d_ = sv[:, kc // 8, :, (kc % 8) * 512:(kc % 8 + 1) * 512].rearrange("g p c -> p g c")
            dma("pool", d_, s, [], ["wscr_" + name], "cv_" + name)

    dma("pool", wf_scr.rearrange("p (k c) -> p k c", c=8),
        wsrc["w_in"][:, 3072:3080].rearrange("(k p) c -> p k c", p=128), [], ["wf_scr"], "cv_f")
    dma("sp", WF[:], wf_scr.rearrange("p (k c) -> p k c", c=8), ["wf_scr"], ["WF"], "wf")

    if WMODE == 0:
        for n in WORDER:
            convert(n)

    def w_src_view(name, ti):
        srcname, c0, K, N, t = WSPEC[name]
        src = wsrc[srcname]
        if t == "S":
            return src[:, c0 + ti * 256:c0 + (ti + 1) * 256].rearrange("(k p) c -> p k c", p=128), 16, 256
        if t == "S8":
            return src[:, c0 + ti * 512:c0 + (ti + 1) * 512].rearrange("(k p) c -> p k c", p=128), 8, 512
        nkh = (K // 128 + 7) // 8
        cg, kh = ti // nkh, ti % nkh
        r0 = kh * 1024
        r1 = min(K, r0 + 1024)
        return src[r0:r1, c0 + cg * 512:c0 + (cg + 1) * 512].rearrange("(k p) c -> p k c", p=128), (r1 - r0) // 128, 512

    def tile_schedule():
        sch = []
        for n in ("ka", "va", "kb", "vb"):
            for cg in range(2):
                for kh in range(2):
                    sch.append((n, cg * 2 + kh))
        for n in ("qa", "qb"):
            for cg in range(4):
                sch.append((n, cg))
        for cg2 in range(4):
            sch += [("za", cg2 * 2), ("za", cg2 * 2 + 1), ("oa", cg2), ("zb", cg2 * 2), ("zb", cg2 * 2 + 1), ("ob", cg2)]
        for cg in range(4):
            for kh in range(2):
                sch.append(("out", cg * 2 + kh))
        for cg in range(22):
            sch.append(("gate", cg))
            sch.append(("up", cg))
        for cg in range(4):
            for kh in range(6):
                sch.append(("down", cg * 6 + kh))
        return sch

    NTILES_ALL = SEQ // 512 + 1
    NSCHED = len(tile_schedule())
    full_sched = []
    for t in range(NTILES_ALL):
        full_sched += tile_schedule()
    wstate = {"next_load": 0, "next_use": 0}

    def w_issue():
        r = wstate["next_load"]
        if r >= len(full_sched):
            return
        name, ti = full_sched[r]
        slot = r % NSLOT
        skey = "wscr:%s:%d" % (name, ti)
        if r < NSCHED and WMODE == 1:
            v, nk, ncol = w_src_view(name, ti)
            dma("pool", WS[slot][:, 0:nk * ncol].rearrange("p (k c) -> p k c", c=ncol), v, [], ["WS%d" % slot], "ws%d" % slot)
            dma("sp", wscr[name][ti], WS[slot][:], ["WS%d" % slot], [skey], "wst%d" % slot)
        else:
            dma("sp", WS[slot][:], wscr[name][ti], [skey, "wscr_" + name], ["WS%d" % slot], "ws%d" % slot)
        wstate["next_load"] = r + 1

    def w_next(name, ti):
        r = wstate["next_use"]
        assert full_sched[r] == (name, ti), (full_sched[r], name, ti)
        wstate["next_use"] = r + 1
        slot = r % NSLOT
        return WS[slot], "WS%d" % slot

    def w_done():
        w_issue()

    for _ in range(NSLOT):
        w_issue()

    def emit_tile(ti, sample):
        if sample:
            nb, bs = 2, TS
            xsrc, ydst = xs, ys
            rows = [1, 2]
        else:
            nb, bs = 4, 128
            xsrc, ydst = xp[ti * 512:(ti + 1) * 512, :], yp[ti * 512:(ti + 1) * 512, :]
            rows = [0, 0, 0, 0]
        NT = nb * bs
        t0 = 0 if sample else ti * 512

        for b in range(nb):
            dma("sp", X[0:bs, b, :], xsrc[b * bs:(b + 1) * bs, :], [], ["X%d" % b], "x%d" % b)
        if not sample and ti == 0:
            load_gates(0, 128)

        def norm_to_hT(vi):
            P.add("dve", lambda e: e.memset(DUMMY[:, 0:1], 0.0), reads=[], writes=["hT", "DUMMY0"])
            def stats(b):
                memset("dve", SSQ[0:bs, b:b + 1], 0.0, ["SSQ%d" % b])
                act(SQJ[0:bs, :], X[0:bs, b, :], AF.Square, ["X%d" % b], ["SQJ", "SSQ%d" % b], accum=SSQ[0:bs, b:b + 1])
                ts("dve", RSTD[0:bs, b:b + 1], SSQ[0:bs, b:b + 1], 1.0 / D, EPS, ALU.mult, ALU.add, ["SSQ%d" % b], ["RSTD%d" % b])
                act(RSTD[0:bs, b:b + 1], RSTD[0:bs, b:b + 1], AF.Sqrt, ["RSTD%d" % b], ["RSTD%d" % b])
                P.add("dve", (lambda b: lambda e: e.reciprocal(RSTD[0:bs, b:b + 1], RSTD[0:bs, b:b + 1]))(b), reads=["RSTD%d" % b], writes=["RSTD%d" % b])
            stats(0)
            for b in range(nb):
                act(XN[0:bs, :], X[0:bs, b, :], AF.Copy, ["X%d" % b, "RSTD%d" % b], ["XN"], scale=RSTD[0:bs, b:b + 1])
                if b + 1 < nb:
                    stats(b + 1)
                r = rows[b]
                for c4 in range(4):
                    pi = next_ps(4)
                    for j in range(4):
                        c = c4 * 4 + j
                        tr(PS[pi][:, j * 128:j * 128 + bs], XN[0:bs, c * 128:(c + 1) * 128], identf[0:bs, 0:bs], ["XN", "identf"], ["PS%d" % pi])
                    for j in range(4):
                        c = c4 * 4 + j
                        if c4 % 2 == 0:
                            act(hT[:, c, b * bs:(b + 1) * bs], PS[pi][:, j * 128:j * 128 + bs], AF.Identity, ["PS%d" % pi, "MODF", "hT"], ["hTc%d" % c],
                                bias=MODF[:, r, vi * 2 + 1, c:c + 1], scale=MODF[:, r, vi * 2, c:c + 1])
                        else:
                            ts("dve", hT[:, c, b * bs:(b + 1) * bs], PS[pi][:, j * 128:j * 128 + bs],
                               MODF[:, r, vi * 2, c:c + 1], MODF[:, r, vi * 2 + 1, c:c + 1], ALU.mult, ALU.add, ["PS%d" % pi, "MODF", "hT"], ["hTc%d" % c])

        norm_to_hT(0)
        P.add("dve", lambda e: e.memset(DUMMY[:, 1:2], 0.0), reads=["hTc%d" % c for c in range(KC)], writes=["hT", "DUMMY1"])

        stg_rot = [0]

        def tokmajor(name, outs):
            for cg in range(2):
                for kh in range(2):
                    wt, wk = w_next(name, cg * 2 + kh)
                    wv = wt[:].rearrange("p (k c) -> p k c", c=512)
                    for b in range(nb):
                        for k8 in range(8):
                            kc = kh * 8 + k8
                            mm(PS[b][0:bs, :], hT[:, kc, b * bs:(b + 1) * bs], wv[:, k8, :], kc == 0, kc == KC - 1,
                               ["hT", wk], ["PS%d" % b])
                    w_done()
                outl = []
                for b in range(nb):
                    si = stg_rot[0] % 2
                    stg_rot[0] += 1
                    sf = STGF[si]
                    cp("dve", sf[0:bs, :], PS[b][0:bs, :], ["PS%d" % b], ["STGF%d" % si])
                    sbf, sbk = next_stgb()
                    cp("act", sbf[0:bs, :], sf[0:bs, :], ["STGF%d" % si], [sbk])
                    if outs is not None and outs[1](b):
                        dma(STQ, outs[0](b)[:, cg * 512:(cg + 1) * 512], sf[0:bs, :], ["STGF%d" % si], [], "STGF%d" % si)
                    outl.append((cg, b, sbf, sbk))
                for item in outl:
                    yield item

        stgb_rot = [0]

        def next_stgb():
            i = stgb_rot[0] % 4
            stgb_rot[0] += 1
            return STGB[i], "STGB%d" % i

        ktb_rot = [0]

        def k_path(name, out_ap, kT_scr, out_rows_ok):
            for cg, b, sbf, sbk in tokmajor(name, (out_ap, out_rows_ok)):
                ki = ktb_rot[0] % 2
                ktb_rot[0] += 1
                ktb = KTB[ki]
                for j in range(4):
                    tr(PB[:, ki * 512 + j * 128:ki * 512 + j * 128 + bs], sbf[0:bs, j * 128:(j + 1) * 128], identb[0:bs, 0:bs], [sbk, "identb"], ["PB"])
                cp("dve", ktb[:, :, 0:bs], PB[:, ki * 512:(ki + 1) * 512].rearrange("p (j t) -> p j t", t=128)[:, :, 0:bs], ["PB"], ["KTB%d" % ki])
                dst = kT_scr[cg * 4:(cg + 1) * 4, :, t0 + b * bs:t0 + (b + 1) * bs].rearrange("h d t -> d h t")
                if sample:
                    dst = kT_scr[cg * 4:(cg + 1) * 4, :, b * 128:b * 128 + bs].rearrange("h d t -> d h t")
                gblk = b if sample else t0 // 128 + b
                dma(STQ, dst, ktb[:, :, 0:bs], ["KTB%d" % ki], ["%s:%d:%d" % (name, gblk, cg)], "KTB%d" % ki)

        def v_path(name, out_ap, v_scr, out_rows_ok):
            for cg, b, sbf, sbk in tokmajor(name, (out_ap, out_rows_ok)):
                if sample:
                    dst = v_scr[b * 128:b * 128 + bs, cg * 512:(cg + 1) * 512]
                else:
                    dst = v_scr[t0 + b * bs:t0 + (b + 1) * bs, cg * 512:(cg + 1) * 512]
                gblk = b if sample else t0 // 128 + b
                dma(STQ, dst, sbf[0:bs, :], [sbk], ["%s:%d:%d" % (name, gblk, cg)], sbk)

        if sample:
            fk_out = lambda b: fks[b * bs:(b + 1) * bs, :]
            fv_out = lambda b: fvs[b * bs:(b + 1) * bs, :]
            bk_out = lambda b: bks[b * bs:(b + 1) * bs, :]
            bv_out = lambda b: bvs[b * bs:(b + 1) * bs, :]
            always = lambda b: True
            band_ok = always
        else:
            fk_out = lambda b: fkp[t0 + b * bs:t0 + (b + 1) * bs, :]
            fv_out = lambda b: fvp[t0 + b * bs:t0 + (b + 1) * bs, :]
            bk_out = lambda b: bkp[b * bs:(b + 1) * bs, :]
            bv_out = lambda b: bvp[b * bs:(b + 1) * bs, :]
            always = lambda b: True
            band_ok = (lambda b: True) if ti == SEQ // 512 - 1 else (lambda b: False)

        k_path("ka", fk_out, kaT_scr, always)
        v_path("va", fv_out, va_scr, always)

        for b in range(nb):
            for kc in range(KC):
                mm(PS[6][0:bs, 0:8], hT[:, kc, b * bs:(b + 1) * bs], WF[:, kc, :], kc == 0, kc == KC - 1, ["hT", "WF"], ["PS6"])
            tt("dve", LTMP[0:bs, :], PS[6][0:bs, 0:8], BFR[0:bs, :], ALU.add, ["PS6", "BFR"], ["LTMP"])
            act(LTMP[0:bs, :], LTMP[0:bs, :], AF.Exp, ["LTMP"], ["LTMP"], scale=-1.0)
            ts("dve", LTMP[0:bs, :], LTMP[0:bs, :], 1.0, 0.0, ALU.add, ALU.add, ["LTMP"], ["LTMP"])
            act(LTMP[0:bs, :], LTMP[0:bs, :], AF.Ln, ["LTMP"], ["LTMP"])
            ts("dve", LFT[0:bs, b, :], LTMP[0:bs, :], -1.0, 0.0, ALU.mult, ALU.add, ["LTMP"], ["LFT"])
        if sample:
            for b in range(nb):
                dma(STQ, fls[b * bs:(b + 1) * bs, :], LFT[0:bs, b, :], ["LFT"], [], "lft")
        else:
            dma(STQ, flp[t0:t0 + 512, :].rearrange("(b p) h -> p b h", p=128), LFT[:, :, :], ["LFT"], [], "lft")
            for b in range(nb):
                gb = ti * 4 + b
                mm(PS[6][:, 8:16], Ltri[:], LFT[:, b, :], True, gb == 0, ["Ltri", "LFT"], ["PS6"])
                if gb > 0:
                    mm(PS[6][:, 8:16], Elast[:], Fall[:, gb - 1, :], False, True, ["Elast", "Fall"], ["PS6"])
                cp("dve", Fall[:, gb, :], PS[6][:, 8:16], ["PS6"], ["Fall"])

        k_path("kb", bk_out, kbT_scr, band_ok)
        v_path("vb", bv_out, vb_scr, band_ok)

        def featmajor(name, ncg, nk, ncol, rhs_of, evac):
            per = ncol // 128
            for cg in range(ncg):
                wt, wk = w_next(name, cg)
                wv = wt[:].rearrange("p (k c) -> p k c", c=ncol)
                for o in range(per):
                    pi = next_ps(4)
                    for k in range(nk):
                        rhs, rk = rhs_of(k)
                        mm(PS[pi][:, 0:NT], wv[:, k, o * 128:(o + 1) * 128], rhs, k == 0, k == nk - 1, [wk, rk], ["PS%d" % pi])
                    evac(cg * per + o, pi)
                w_done()

        h_rhs = lambda k: (hT[:, k, 0:NT], "hT")

        def q_evac(dst, key):
            def f(oc, pi):
                act(dst[:, oc, 0:NT], PS[pi][:, 0:NT], AF.Copy, ["PS%d" % pi], K_(key), scale=INV)
            return f
        featmajor("qa", 4, KC, 256, h_rhs, q_evac(QaT, "QaT"))
        featmajor("qb", 4, KC, 256, h_rhs, q_evac(QbT, "QbT"))

        pt_rot = [0]
        ks_rot = [0]

        def attn_head(h, QT, qkey, oT, okey, blocks, band):
            po = (h % 2) * 2
            pO, pD = PS[po], PS[po + 1]
            kO, kD = "PS%d" % po, "PS%d" % (po + 1)
            n = len(blocks)
            pend = []

            def issue_s(i):
                bl = blocks[i]
                pi = 4 + (i % 3)
                nk, c0, c1 = bl["nk"], bl["c0"], bl["c1"]
                kt, ktk = bl["kt"]
                mm(PS[pi][0:nk, c0:c1], kt, QT[:, h, c0:c1], True, not band, [ktk, qkey], ["PS%d" % pi])
                if band:
                    u0 = bl["u0"]
                    mm(PS[pi][0:nk, c0:c1], identb[0:nk, 0:nk], BH[0:nk, h, u0:u0 + (c1 - c0)], False, True, ["identb", "BH"], ["PS%d" % pi])
                return pi

            pis = {}
            pis[0] = issue_s(0)
            if n > 1:
                pis[1] = issue_s(1)
            for i in range(n):
                if i + 2 < n:
                    pis[i + 2] = issue_s(i + 2)
                bl = blocks[i]
                pi = pis[i]
                nk, c0, c1 = bl["nk"], bl["c0"], bl["c1"]
                pti = pt_rot[0] % 3
                pt_rot[0] += 1
                pt = PT[pti]
                ptk = "PT%d" % pti
                if bl.get("bias") is not None:
                    act(pt[0:nk, c0:c1], PS[pi][0:nk, c0:c1], AF.Exp, ["PS%d" % pi, "NB"], [ptk], bias=bl["bias"])
                else:
                    act(pt[0:nk, c0:c1], PS[pi][0:nk, c0:c1], AF.Exp, ["PS%d" % pi], [ptk])
                if bl.get("mask"):
                    w = bl["mask"]
                    tt("dve", pt[0:nk, c0:c0 + w], pt[0:nk, c0:c0 + w], Ub[0:nk, 0:w], ALU.mult, [ptk, "Ub"], [ptk])
                v, vk = bl["v"]
                mm(pO[:, c0:c1], v, pt[0:nk, c0:c1], i == 0, i == n - 1, [vk, ptk], [kO])
                mm(pD[:, c0:c1], onesb[0:nk, :], pt[0:nk, c0:c1], i == 0, i == n - 1, ["onesb", ptk], [kD])
            P.add("dve", (lambda pD: lambda e: e.reciprocal(RDEN[:, 0:NT], pD[:, 0:NT]))(pD), reads=[kD], writes=["RDEN"])
            tt("dve", oT[:, h, 0:NT], pO[:, 0:NT], RDEN[:, 0:NT], ALU.mult, [kO, "RDEN"], [okey])

        def stream_kv(h, kname, vname, kT_scr, v_scr, key0, nkeys):
            si = ks_rot[0] % 3
            ks_rot[0] += 1
            ks, vs = KS[si], VS[si]
            nblk = nkeys // 128
            kkeys = ["%s:%d:%d" % (kname, key0 // 128 + j, h // 4) for j in range(nblk)]
            vkeys = ["%s:%d:%d" % (vname, key0 // 128 + j, h // 4) for j in range(nblk)]
            dma("sp", ks[:, 0:nkeys], kT_scr[h, :, key0:key0 + nkeys], kkeys, K_("KS%d" % si), "ks%d" % si)
            dma("sp", vs[:, 0:nblk, :], v_scr[key0:key0 + nkeys, h * 128:(h + 1) * 128].rearrange("(b p) d -> p b d", p=128),
                vkeys, K_("VS%d" % si), "vs%d" % si)
            return ks, "KS%d" % si, vs, "VS%d" % si


        def sample_attention():
            CKs = [(KS[0], "KS0"), (KS[1], "KS1")]
            CVs = [(VS[i_].rearrange("p a b -> p (a b)"), "VS%d" % i_) for i_ in range(3)]
            KTSs = [(KS[2].rearrange("p (a b) -> p a b", a=8), "KS2")]
            rot = [0]
            for b in range(2):
                q0 = b * TS
                pO, pD = PS[b * 2], PS[b * 2 + 1]
                kO, kD = "PS%d" % (b * 2), "PS%d" % (b * 2 + 1)
                dma("sp", CL[:], cfl[b].rearrange("(k p) h -> p k h", p=128), [], ["CL"], "cl")
                for kb in range(32):
                    mm(PS[6][:, 8:16], Ltri[:], CL[:, kb, :], True, kb == 0, ["Ltri", "CL"], ["PS6"])
                    if kb > 0:
                        mm(PS[6][:, 8:16], Elast[:], Fall[:, kb - 1, :], False, True, ["Elast", "Fall"], ["PS6"])
                    cp("dve", Fall[:, kb, :], PS[6][:, 8:16], ["PS6"], ["Fall"])
                mm(PS[6][0:TS, 8:16], Ltri[0:TS, 0:TS], LFT[0:TS, b, :], True, False, ["Ltri", "LFT"], ["PS6"])
                mm(PS[6][0:TS, 8:16], Elast[:, 0:TS], Fall[:, 31, :], False, True, ["Elast", "Fall"], ["PS6"])
                cp("dve", Fall[0:TS, 32, :], PS[6][0:TS, 8:16], ["PS6"], ["Fall"])
                mm(PS[6][:, 16:24], E31[0:TS, :], Fall[0:TS, 32, :], True, True, ["E31", "Fall"], ["PS6"])
                cp("dve", Fref[:], PS[6][:, 16:24], ["PS6"], ["Fref"])
                for kb in range(32):
                    tt("dve", NB[:, kb, :], Fref[:], Fall[:, kb, :], ALU.subtract, ["Fref", "Fall"], ["NB"])
                tt("dve", NB[0:TS, 32, :], Fref[0:TS, :], Fall[0:TS, 32, :], ALU.subtract, ["Fref", "Fall"], ["NB"])

                def run_attn(band, ncache, ck_src, cv_src, kname, vname, kT_scr, v_scr, QT, qkey, oT, okey):
                    nblk = ncache + 1

                    def stage1(kb):
                        i = rot[0]
                        rot[0] += 1
                        (ck, ckk), (cv, cvk), (kts, ktsk) = CKs[i % 2], CVs[i % 3], KTSs[0]
                        new = kb == ncache
                        nk = TS if new else 128
                        if not new:
                            dma("pool", ck[:, 0:1024], ck_src[b, kb * 128:(kb + 1) * 128, :], [], K_(ckk), "s" + ckk)
                            dma("pool", cv[:, 0:1024], cv_src[b, kb * 128:(kb + 1) * 128, :], [], K_(cvk), "s" + cvk)
                            for h in range(NH):
                                tr(PB[:, h * 128:(h + 1) * 128], ck[:, h * 128:(h + 1) * 128], identb[:], [ckk, "identb"], ["PB"])
                            cp("dve", kts[:, :, :], PB[:, :].rearrange("p (h t) -> p h t", h=8), ["PB"], K_(ktsk))
                        else:
                            kkeys = ["%s:%d:%d" % (kname, b, c) for c in range(2)]
                            vkeys = ["%s:%d:%d" % (vname, b, c) for c in range(2)]
                            dma("sp", kts[:, :, 0:TS], kT_scr[:, :, b * 128:b * 128 + TS].rearrange("h d t -> d h t"), kkeys, K_(ktsk), "s" + ktsk)
                            dma("sp", cv[0:TS, 0:1024], v_scr[b * 128:b * 128 + TS, :], vkeys, K_(cvk), "s" + cvk)
                        pi = 4 + (kb % 2)
                        for h in range(NH):
                            mm(PS[pi][0:nk, h * TS:(h + 1) * TS], kts[:, h, 0:nk], QT[:, h, q0:q0 + TS], True, not band, [ktsk, qkey], ["PS%d" % pi])
                            if band:
                                u0 = 0 if new else 512 - 128 * kb
                                mm(PS[pi][0:nk, h * TS:(h + 1) * TS], identb[0:nk, 0:nk], BH[0:nk, h, u0:u0 + TS], False, True, ["identb", "BH"], ["PS%d" % pi])
                        return (kb, new, nk, pi, cv, cvk)

                    def stage2(st):
                        kb, new, nk, pi, cv, cvk = st
                        pti = pt_rot[0] % 3
                        pt_rot[0] += 1
                        pt, ptk = PT[pti], "PT%d" % pti
                        hk = ["%s:h%d" % (ptk, h) for h in range(NH)]
                        if band:
                            act(pt[0:nk, 0:256], PS[pi][0:nk, 0:256], AF.Exp, ["PS%d" % pi, ptk], hk)
                        else:
                            for h in range(NH):
                                act(pt[0:nk, h * TS:(h + 1) * TS], PS[pi][0:nk, h * TS:(h + 1) * TS], AF.Exp, ["PS%d" % pi, "NB", ptk], [hk[h]],
                                    bias=NB[0:nk, kb, h:h + 1])
                            if new:
                                for h in range(NH):
                                    tt("dve", pt[0:nk, h * TS:(h + 1) * TS], pt[0:nk, h * TS:(h + 1) * TS], Ub[0:TS, 0:TS], ALU.mult, [hk[h], "Ub"], [hk[h]])
                        for h in range(NH):
                            mm(pO[:, h * TS:(h + 1) * TS], cv[0:nk, h * 128:(h + 1) * 128], pt[0:nk, h * TS:(h + 1) * TS], kb == 0, kb == nblk - 1, [cvk, hk[h]], [kO])
                        mm(pD[:, 0:256], onesb[0:nk, :], pt[0:nk, 0:256], kb == 0, kb == nblk - 1, ["onesb"] + hk, [kD])

                    st = stage1(0)
                    for kb in range(nblk):
                        nxt = stage1(kb + 1) if kb + 1 < nblk else None
                        stage2(st)
                        st = nxt
                    P.add("dve", (lambda pD: lambda e: e.reciprocal(RDEN[:, 0:256], pD[:, 0:256]))(pD), reads=[kD], writes=["RDEN"])
                    tt("dve", oT[:, :, q0:q0 + TS], pO[:, 0:256].rearrange("p (h t) -> p h t", h=8), RDEN[:, 0:256].rearrange("p (h t) -> p h t", h=8),
                       ALU.mult, [kO, "RDEN"], [okey])

                run_attn(False, 32, cfk, cfv, "ka", "va", kaT_scr, va_scr, QaT, "QaT", oaT, "oaT")
                run_attn(True, 4, cbk, cbv, "kb", "vb", kbT_scr, vb_scr, QbT, "QbT", obT, "obT")

        if not sample:
            mm(PS[6][:, 16:24], Elast[:], Fall[:, ti * 4 + 3, :], True, True, ["Elast", "Fall"], ["PS6"])
            cp("dve", Fref[:], PS[6][:, 16:24], ["PS6"], ["Fref"])
            nkb = ti * 4 + 4
            for kb in range(nkb):
                tt("dve", NB[:, kb, :], Fref[:], Fall[:, kb, :], ALU.subtract, ["Fref", "Fall"], ["NB"])
            for h in range(NH):
                blocks = []
                for c0k in range(0, nkb * 128, 1024):
                    nkeys = min(1024, nkb * 128 - c0k)
                    ks, ksk, vs, vsk = stream_kv(h, "ka", "va", kaT_scr, va_scr, c0k, nkeys)
                    for j in range(nkeys // 128):
                        kb = c0k // 128 + j
                        jb = kb - ti * 4
                        bl = dict(kt=(ks[:, j * 128:(j + 1) * 128], ksk), v=(vs[:, j, :], vsk), nk=128,
                                  c0=0, c1=512, bias=NB[:, kb, h:h + 1])
                        if jb >= 0:
                            bl["c0"] = jb * 128
                            bl["mask"] = 128
                        blocks.append(bl)
                attn_head(h, QaT, "QaT", oaT, "oaT", blocks, False)
            for h in range(NH):
                gk0 = max(0, ti * 4 - 4)
                nkeys = (ti * 4 + 4 - gk0) * 128
                ks, ksk, vs, vsk = stream_kv(h, "kb", "vb", kbT_scr, vb_scr, gk0 * 128, nkeys)
                blocks = []
                order = [4, 5, 6, 7, 3, 2, 1, 0]
                for m in order:
                    gkb = ti * 4 - 4 + m
                    if gkb < 0:
                        continue
                    j = gkb - gk0
                    dl = 8 - 2 * m
                    jlo = max(0, -dl)
                    jhi = min(7, 9 - dl)
                    c0, c1 = jlo * 64, (jhi + 1) * 64
                    blocks.append(dict(kt=(ks[:, j * 128:(j + 1) * 128], ksk), v=(vs[:, j, :], vsk), nk=128,
                                       c0=c0, c1=c1, u0=dl * 64 + c0))
                attn_head(h, QbT, "QbT", obT, "obT", blocks, True)
        else:
            sample_attention()

        for cg2 in range(4):
            for which, (zname, oname, oT_, okey) in enumerate((("za", "oa", oaT, "oaT"), ("zb", "ob", obT, "obT"))):
                zps = []
                for half in range(2):
                    wt, wk = w_next(zname, cg2 * 2 + half)
                    wv = wt[:].rearrange("p (k c) -> p k c", c=256)
                    for o in range(2):
                        pi = len(zps)
                        for k in range(KC):
                            mm(PS[pi][:, 0:NT], wv[:, k, o * 128:(o + 1) * 128], hT[:, k, 0:NT], k == 0, k == KC - 1, [wk, "hT"], ["PS%d" % pi])
                        act(SGS[pi][:, 0:NT], PS[pi][:, 0:NT], AF.Sigmoid, ["PS%d" % pi], ["SG%d" % pi])
                        zps.append(pi)
                    w_done()
                wt, wk = w_next(oname, cg2)
                wv = wt[:].rearrange("p (k c) -> p k c", c=512)
                for o in range(4):
                    oc = cg2 * 4 + o
                    pi = 4 + (o % 2)
                    for k in range(8):
                        mm(PS[pi][:, 0:NT], wv[:, k, o * 128:(o + 1) * 128], oT_[:, k, 0:NT], k == 0, k == 7, [wk, okey], ["PS%d" % pi])
                    if which == 0:
                        tt("dve", MAS[o][:, 0:NT], PS[pi][:, 0:NT], SGS[o][:, 0:NT], ALU.mult, ["PS%d" % pi, "SG%d" % o], ["MA%d" % o])
                    else:
                        tt("dve", MTMP[:, 0:NT], PS[pi][:, 0:NT], SGS[o][:, 0:NT], ALU.mult, ["PS%d" % pi, "SG%d" % o], ["MTMP"])
                        tt("dve", mT[:, oc, 0:NT], MTMP[:, 0:NT], MAS[o][:, 0:NT], ALU.add, ["MTMP", "MA%d" % o], K_("mT"))
                w_done()

        def resid_pass(name, nkh, lhs_of, gi):
            if sample:
                dma("sp", XN[0:bs, :], adaflat[2:3, (2 + 3 * gi) * D:(3 + 3 * gi) * D].partition_broadcast(bs), ["adascr"], ["XN"], "xng")
            for cg in range(4):
                for kh in range(nkh):
                    wt, wk = w_next(name, cg * nkh + kh)
                    wv = wt[:].rearrange("p (k c) -> p k c", c=512)
                    nk8 = 8 if (name != "down" or kh < 5) else 4
                    for b in range(nb):
                        for k8 in range(nk8):
                            kc = kh * 8 + k8
                            lhs, lk = lhs_of(kc, b)
                            last = (kh == nkh - 1 and k8 == nk8 - 1)
                            mm(PS[b][0:bs, :], lhs, wv[:, k8, :], kc == 0, last, K_(lk) + [wk], ["PS%d" % b])
                    w_done()
                for b in range(nb):
                    if sample and b == 1:
                        gsrc = XN[0:bs, cg * 512:(cg + 1) * 512]
                        gk = "XN"
                    else:
                        gsrc = GREP[0:bs, gi, cg * 512:(cg + 1) * 512]
                        gk = "GREP"
                    tt("dve", MTMP2[0:bs, :], PS[b][0:bs, :], gsrc, ALU.mult, ["PS%d" % b, gk], ["MTMP2"])
                    tt("dve", X[0:bs, b, cg * 512:(cg + 1) * 512], X[0:bs, b, cg * 512:(cg + 1) * 512], MTMP2[0:bs, :], ALU.add,
                       ["X%d" % b, "MTMP2"], ["X%d" % b])

        resid_pass("out", 2, lambda kc, b: (mT[:, kc, b * bs:(b + 1) * bs], "mT"), 0)

        norm_to_hT(1)
        P.add("dve", lambda e: e.memset(DUMMY[:, 1:2], 0.0), reads=["hTc%d" % c for c in range(KC)], writes=["hT", "DUMMY1"])
        for cg in range(22):
            wtg, wkg = w_next("gate", cg)
            wtu, wku = w_next("up", cg)
            wvg = wtg[:].rearrange("p (k c) -> p k c", c=256)
            wvu = wtu[:].rearrange("p (k c) -> p k c", c=256)
            for o in range(2):
                fc = cg * 2 + o
                pg = next_ps(4)
                for k in range(KC):
                    mm(PS[pg][:, 0:NT], wvg[:, k, o * 128:(o + 1) * 128], hT[:, k, 0:NT], k == 0, k == KC - 1, [wkg, "hT"], ["PS%d" % pg])
                pu = next_ps(4)
                for k in range(KC):
                    mm(PS[pu][:, 0:NT], wvu[:, k, o * 128:(o + 1) * 128], hT[:, k, 0:NT], k == 0, k == KC - 1, [wku, "hT"], ["PS%d" % pu])
                sgi = fc % 4
                act(SGS[sgi][:, 0:NT], PS[pg][:, 0:NT], AF.Silu, ["PS%d" % pg], ["SG%d" % sgi])
                tt("dve", actT[:, fc, 0:NT], PS[pu][:, 0:NT], SGS[sgi][:, 0:NT], ALU.mult, ["PS%d" % pu, "SG%d" % sgi], K_("actT"))
            w_done()
            w_done()
        resid_pass("down", 6, lambda kc, b: (actT[:, kc, b * bs:(b + 1) * bs], "actT"), 1)

        for b in range(nb):
            memset("dve", SSQ[0:bs, b:b + 1], 0.0, ["SSQ%d" % b])
            act(SQJ[0:bs, :], X[0:bs, b, :], AF.Square, ["X%d" % b], ["SQJ", "SSQ%d" % b], accum=SSQ[0:bs, b:b + 1])
            ts("dve", RSTD[0:bs, b:b + 1], SSQ[0:bs, b:b + 1], 1.0 / D, EPS, ALU.mult, ALU.add, ["SSQ%d" % b], ["RSTD%d" % b])
            act(RSTD[0:bs, b:b + 1], RSTD[0:bs, b:b + 1], AF.Sqrt, ["RSTD%d" % b], ["RSTD%d" % b])
            P.add("dve", (lambda b: lambda e: e.reciprocal(RSTD[0:bs, b:b + 1], RSTD[0:bs, b:b + 1]))(b), reads=["RSTD%d" % b], writes=["RSTD%d" % b])
            P.add("dve", (lambda b: lambda e: e.scalar_tensor_tensor(X[0:bs, b, :], X[0:bs, b, :], RSTD[0:bs, b:b + 1], GFIN[0:bs, :], ALU.mult, ALU.mult))(b),
                  reads=["X%d" % b, "RSTD%d" % b, "GFIN"], writes=["X%d" % b])
            dma(STQ, ydst[b * bs:(b + 1) * bs, :], X[0:bs, b, :], ["X%d" % b], [], "x%d" % b)


    for ti in range(ntiles):
        emit_tile(ti, False)
    if do_sample:
        load_gates(1, TS)
        emit_tile(SEQ // 512, True)

    P.emit()
    return nc


_NC_CACHE = {}


def kernel(x_prompt, x_sample, cache_fox_k, cache_fox_v, cache_fox_logf, cache_band_k, cache_band_v,
           c_prompt, c_sample, w_ada, b_ada, g_mix, w_in, b_f, rel_bias, w_oa, w_ob, w_out,
           g_ffn, w_gate, w_up, w_down, g_final):
    f = lambda a: np.ascontiguousarray(np.asarray(a, dtype=np.float32))
    if "nc" not in _NC_CACHE:
        _NC_CACHE["nc"] = build_program()
    nc = _NC_CACHE["nc"]
    shared = {
        "w_ada": f(w_ada)[0], "b_ada": f(b_ada)[0].reshape(96, 128), "g_mix": f(g_mix)[0].reshape(16, 128),
        "g_ffn": f(g_ffn)[0].reshape(16, 128), "g_fin": f(g_final).reshape(1, D), "b_f": f(b_f)[0].reshape(1, NH),
        "relb": f(rel_bias)[0], "w_in": f(w_in)[0], "w_oa": f(w_oa)[0], "w_ob": f(w_ob)[0], "w_out": f(w_out)[0],
        "w_gate": f(w_gate)[0], "w_up": f(w_up)[0], "w_down": f(w_down)[0],
    }
    xpf, xsf = f(x_prompt), f(x_sample)
    cfkf, cfvf, cflf = f(cache_fox_k)[0], f(cache_fox_v)[0], f(cache_fox_logf)[0]
    cbkf, cbvf = f(cache_band_k)[0], f(cache_band_v)[0]
    cpf, csf = f(c_prompt), f(c_sample)
    in_maps = []
    for i in range(NCORES):
        m = dict(shared)
        m["xp"] = xpf[i]
        m["xs"] = xsf[2 * i:2 * i + 2].reshape(2 * TS, D)
        m["cfk"] = cfkf[2 * i:2 * i + 2].reshape(2, PAST, NH * HD)
        m["cfv"] = cfvf[2 * i:2 * i + 2].reshape(2, PAST, NH * HD)
        m["cfl"] = cflf[2 * i:2 * i + 2]
        m["cbk"] = cbkf[2 * i:2 * i + 2].reshape(2, LB, NH * HD)
        m["cbv"] = cbvf[2 * i:2 * i + 2].reshape(2, LB, NH * HD)
        m["c3"] = np.ascontiguousarray(np.concatenate([cpf[i:i + 1], csf[2 * i:2 * i + 2]], axis=0))
        in_maps.append(m)
    res = run_bass_kernel_spmd(nc, in_maps, core_ids=list(range(NCORES)))
    R = res.results
    cat = lambda k: np.stack([np.asarray(r[k]) for r in R], axis=0)
    y_prompt = cat("yp").reshape(8, SEQ, D)
    y_sample = cat("ys").reshape(16, TS, D)
    fk_p = cat("fkp").reshape(1, 8, SEQ, NH, HD)
    fv_p = cat("fvp").reshape(1, 8, SEQ, NH, HD)
    fl_p = cat("flp").reshape(1, 8, SEQ, NH)
    bk_p = cat("bkp").reshape(1, 8, LB, NH, HD)
    bv_p = cat("bvp").reshape(1, 8, LB, NH, HD)
    fk_s = cat("fks").reshape(1, 16, TS, NH, HD)
    fv_s = cat("fvs").reshape(1, 16, TS, NH, HD)
    fl_s = cat("fls").reshape(1, 16, TS, NH)
    bk_s = cat("bks").reshape(1, 16, TS, NH, HD)
    bv_s = cat("bvs").reshape(1, 16, TS, NH, HD)
    return (y_prompt, y_sample, fk_p, fv_p, fl_p, bk_p, bv_p, fk_s, fv_s, fl_s, bk_s, bv_s)
```

```python
import numpy as np
import concourse.bass as bass
import concourse.mybir as mybir
from concourse.bass_utils import run_bass_kernel_spmd

F32 = mybir.dt.float32
BF16 = mybir.dt.bfloat16
AF = mybir.ActivationFunctionType
ALU = mybir.AluOpType

D = 2048
NH = 8
HD = 128
DFF = 5632
SEQ = 4096
PAST = 4096
LB = 512
TS = 32
NCORES = 8
KC = D // 128
FC = DFF // 128
INV = 1.0 / float(np.sqrt(HD))
NEGB = -30000.0
EPS = 1e-6
STQ = "pool"
WMODE = 1


class _Op:
    __slots__ = ("eng", "fn", "deps", "seq", "sig", "dsem", "dval", "idx")


class Prog:
    ENGS = ("pe", "act", "dve", "pool", "sp")
    SAME_ENG_SYNC = {"pe": False, "act": True, "dve": True, "pool": True, "sp": False}

    def __init__(self, nc):
        self.nc = nc
        self.ops = {e: [] for e in self.ENGS}
        self.last_w = {}
        self.readers = {}
        self.dma_cnt = {}
        self.nops = 0

    def add(self, eng, fn, reads=(), writes=(), dsem=None):
        op = _Op()
        op.eng = eng
        op.fn = fn
        op.sig = False
        op.seq = None
        op.dsem = dsem
        op.idx = self.nops
        self.nops += 1
        deps = {}

        def dep_on(d):
            if d is None:
                return
            if d.dsem is not None:
                deps[("d", id(d.dsem))] = (d.dsem, self.dma_cnt[id(d.dsem)])
            else:
                if d.eng == eng and not self.SAME_ENG_SYNC[eng]:
                    return
                d.sig = True
                key = ("e", d.eng)
                if key not in deps or deps[key].idx < d.idx:
                    deps[key] = d
        for k in reads:
            dep_on(self.last_w.get(k))
            if k.startswith("PS") or k.startswith("PB"):
                for r in self.readers.get(k, ()):
                    if r.eng != eng:
                        dep_on(r)
        for k in writes:
            dep_on(self.last_w.get(k))
            for r in self.readers.get(k, ()):
                dep_on(r)
        for k in reads:
            self.readers.setdefault(k, []).append(op)
        for k in writes:
            self.last_w[k] = op
            self.readers[k] = []
        if dsem is not None:
            self.dma_cnt[id(dsem)] = self.dma_cnt.get(id(dsem), 0) + 16
            op.dval = self.dma_cnt[id(dsem)]
        op.deps = list(deps.values())
        self.ops[eng].append(op)
        return op

    def emit(self):
        nc = self.nc
        esem = {e: nc.alloc_semaphore("es_" + e) for e in self.ENGS}
        for e in self.ENGS:
            c = 0
            for op in self.ops[e]:
                if op.dsem is None and op.sig:
                    c += 1
                    op.seq = c
        ops = self.ops
        all_dsems = {}
        for e in self.ENGS:
            for op in ops[e]:
                if op.dsem is not None:
                    all_dsems[id(op.dsem)] = op.dsem
        dma_cnt = self.dma_cnt

        def run(e, eng):
            waited = {}
            for op in ops[e]:
                for d in op.deps:
                    if isinstance(d, tuple):
                        sem, val = d
                    else:
                        sem, val = esem[d.eng], d.seq
                    if waited.get(id(sem), 0) >= val:
                        continue
                    waited[id(sem)] = val
                    eng.wait_ge(sem, val)
                ins = op.fn(eng)
                if op.dsem is not None:
                    ins.then_inc(op.dsem, 16)
                elif op.sig:
                    ins.then_inc(esem[e], 1)
            if e == "sp":
                for sid, sem in all_dsems.items():
                    if waited.get(sid, 0) < dma_cnt[sid]:
                        eng.wait_ge(sem, dma_cnt[sid])

        with nc.Block() as block:
            @block.tensor
            def _(eng):
                run("pe", eng)

            @block.scalar
            def _(eng):
                run("act", eng)

            @block.vector
            def _(eng):
                run("dve", eng)

            @block.gpsimd
            def _(eng):
                run("pool", eng)

            @block.sync
            def _(eng):
                run("sp", eng)


WSPEC = {
    "qa": ("w_in", 0, D, 1024, "S"),
    "ka": ("w_in", 1024, D, 1024, "A"),
    "va": ("w_in", 2048, D, 1024, "A"),
    "qb": ("w_in", 3080, D, 1024, "S"),
    "kb": ("w_in", 4104, D, 1024, "A"),
    "vb": ("w_in", 5128, D, 1024, "A"),
    "za": ("w_in", 6152, D, 2048, "S"),
    "zb": ("w_in", 8200, D, 2048, "S"),
    "oa": ("w_oa", 0, 1024, 2048, "S8"),
    "ob": ("w_ob", 0, 1024, 2048, "S8"),
    "out": ("w_out", 0, D, 2048, "A"),
    "gate": ("w_gate", 0, D, DFF, "S"),
    "up": ("w_up", 0, D, DFF, "S"),
    "down": ("w_down", 0, DFF, 2048, "A"),
}
WORDER = ["ka", "va", "kb", "vb", "qa", "qb", "za", "oa", "zb", "ob", "out", "gate", "up", "down"]


def _ntiles(name):
    _, _, K, N, t = WSPEC[name]
    if t == "S":
        return N // 256
    if t == "S8":
        return N // 512
    nkh = (K // 128 + 7) // 8
    return (N // 512) * nkh


def build_program(ntiles=SEQ // 512, do_sample=True):
    nc = bass.Bass("TRN2", target_bir_lowering=False)
    P = Prog(nc)

    def din(name, shape, dt=F32):
        return nc.dram_tensor(name, list(shape), dt, kind="ExternalInput").ap()

    def dout(name, shape, dt=F32):
        return nc.dram_tensor(name, list(shape), dt, kind="ExternalOutput").ap()

    def dscr(name, shape, dt):
        return nc.dram_tensor(name, list(shape), dt, kind="Internal").ap()

    xp = din("xp", [SEQ, D])
    xs = din("xs", [2 * TS, D])
    cfk = din("cfk", [2, PAST, NH * HD])
    cfv = din("cfv", [2, PAST, NH * HD])
    cfl = din("cfl", [2, PAST, NH])
    cbk = din("cbk", [2, LB, NH * HD])
    cbv = din("cbv", [2, LB, NH * HD])
    c3 = din("c3", [3, D])
    w_ada = din("w_ada", [D, 6 * D])
    b_ada = din("b_ada", [96, 128])
    g_mix = din("g_mix", [16, 128])
    g_ffn = din("g_ffn", [16, 128])
    g_fin = din("g_fin", [1, D])
    b_f = din("b_f", [1, NH])
    relb = din("relb", [257, NH])
    wsrc = {
        "w_in": din("w_in", [D, 10248]),
        "w_oa": din("w_oa", [1024, D]),
        "w_ob": din("w_ob", [1024, D]),
        "w_out": din("w_out", [D, D]),
        "w_gate": din("w_gate", [D, DFF]),
        "w_up": din("w_up", [D, DFF]),
        "w_down": din("w_down", [DFF, D]),
    }
    yp = dout("yp", [SEQ, D])
    ys = dout("ys", [2 * TS, D])
    fkp = dout("fkp", [SEQ, 1024])
    fvp = dout("fvp", [SEQ, 1024])
    flp = dout("flp", [SEQ, NH])
    bkp = dout("bkp", [LB, 1024])
    bvp = dout("bvp", [LB, 1024])
    fks = dout("fks", [2 * TS, 1024])
    fvs = dout("fvs", [2 * TS, 1024])
    fls = dout("fls", [2 * TS, NH])
    bks = dout("bks", [2 * TS, 1024])
    bvs = dout("bvs", [2 * TS, 1024])

    wscr = {n: dscr("ws_" + n, [_ntiles(n), 128, 4096], BF16) for n in WSPEC}
    wf_scr = dscr("ws_f", [128, 16 * 8], BF16)
    kaT_scr = dscr("kaT_scr", [NH, 128, SEQ], BF16)
    va_scr = dscr("va_scr", [SEQ, 1024], BF16)
    kbT_scr = dscr("kbT_scr", [NH, 128, SEQ], BF16)
    vb_scr = dscr("vb_scr", [SEQ, 1024], BF16)
    adascr = dscr("adascr", [3, 96, 128], F32)
    extscr = dscr("extscr", [NH, 768], F32)

    off = [0]
    base0 = None

    def sb(name, shape, dt):
        return nc.alloc_sbuf_tensor(name, list(shape), dt)

    X = sb("X", [128, 4, D], F32)
    XN = sb("XN", [128, D], F32)
    hT = sb("hT", [128, KC, 512], BF16)
    NSLOT = 3
    WS = [sb("WS%d" % i, [128, 4096], BF16) for i in range(NSLOT)]
    GREP = sb("GREP", [128, 2, D], F32)
    GFIN = sb("GFIN", [128, D], F32)
    BH = sb("BH", [128, NH, 640], BF16)
    MODF = sb("MODF", [128, 3, 4, 16], F32)
    ADAT = sb("ADAT", [128, 96, 3], F32)
    identf = sb("identf", [128, 128], F32)
    identb = sb("identb", [128, 128], BF16)
    Ub = sb("Ub", [128, 128], BF16)
    Ltri = sb("Ltri", [128, 128], F32)
    Elast = sb("Elast", [128, 128], F32)
    Jf = sb("Jf", [128, 128], F32)
    E31 = sb("E31", [128, 128], F32)
    CL = sb("CL", [128, 32, NH], F32)
    onesb = sb("onesb", [128, 128], BF16)
    Fall = sb("Fall", [128, 34, NH], F32)
    Fref = sb("Fref", [128, NH], F32)
    NB = sb("NB", [128, 34, NH], F32)
    WF = sb("WF", [128, 16, 8], BF16)
    BFR = sb("BFR", [128, NH], F32)
    SSQ = sb("SSQ", [128, 4], F32)
    RSTD = sb("RSTD", [128, 4], F32)
    LFT = sb("LFT", [128, 4, NH], F32)
    LTMP = sb("LTMP", [128, NH], F32)
    SMALL = sb("SMALL", [128, 128], F32)
    DUMMY = sb("DUMMY", [128, 4], F32)
    SQJ = sb("SQJ", [128, D], BF16)
    PT = [sb("PT%d" % i, [128, 512], BF16) for i in range(3)]
    STGF = [sb("STGF%d" % i, [128, 512], F32) for i in range(2)]
    STGB = [sb("STGB%d" % i, [128, 512], BF16) for i in range(4)]
    KTB = [sb("KTB%d" % i, [128, 4, 128], BF16) for i in range(2)]
    RDEN = sb("RDEN", [128, 512], F32)
    SGS = [sb("SG%d" % i, [128, 512], F32) for i in range(4)]
    SG = SGS[0]
    MAS = [sb("MA%d" % i, [128, 512], F32) for i in range(4)]
    MTMP = sb("MTMP", [128, 512], F32)
    MTMP2 = sb("MTMP2", [128, 512], F32)
    ARENA = sb("ARENA", [128, FC * 512], BF16)
    abase = None

    def aview(name, elem_off, shape):
        n = 1
        for s in shape:
            n *= s
        flat = ARENA[:, elem_off:elem_off + n]
        if len(shape) == 1:
            return flat
        if len(shape) == 2:
            return flat.rearrange("p (a b) -> p a b", a=shape[0])
        raise ValueError

    actT = aview("actT", 0, [FC, 512])
    mT = aview("mT", 0, [KC, 512])
    QaT = aview("QaT", 0, [NH, 512])
    QbT = aview("QbT", 4096, [NH, 512])
    oaT = aview("oaT", 8192, [NH, 512])
    obT = aview("obT", 12288, [NH, 512])
    KS = [aview("KS%d" % i, 16384 + i * 1024, [1024]) for i in range(3)]
    VS = [aview("VS%d" % i, 19456 + i * 1024, [8, 128]) for i in range(3)]
    ALIAS = {
        "mT": ["QaT", "QbT"],
        "actT": ["mT", "QaT", "QbT", "oaT", "obT", "KS0", "KS1", "KS2", "VS0", "VS1", "VS2"],
    }

    def K_(name):
        return [name] + ALIAS.get(name, [])

    PS = [nc.alloc_psum_tensor("PS%d" % i, [128, 512], F32) for i in range(7)]
    PB = nc.alloc_psum_tensor("PB", [128, 1024], BF16)

    sems = {}

    def S_(name):
        if name not in sems:
            sems[name] = nc.alloc_semaphore("d_" + name)
        return sems[name]

    def dma(q, out, in_, reads, writes, sem, **kw):
        P.add(q, (lambda o, i, kw: lambda e: e.dma_start(out=o, in_=i, **kw))(out, in_, kw),
              reads=reads, writes=writes, dsem=S_(sem))

    def mm(out, lhsT, rhs, start, stop, reads, writes):
        P.add("pe", (lambda o, l, r, s0, s1: lambda e: e.matmul(o, lhsT=l, rhs=r, start=s0, stop=s1))(out, lhsT, rhs, start, stop),
              reads=reads, writes=writes)

    def tr(out, in_, ident, reads, writes):
        P.add("pe", (lambda o, i, d: lambda e: e.transpose(o, i, d))(out, in_, ident), reads=reads, writes=writes)

    def act(out, in_, func, reads, writes, bias=None, scale=None, accum=None):
        kw = {}
        if bias is not None:
            kw["bias"] = bias
        if scale is not None:
            kw["scale"] = scale
        if accum is not None:
            kw["accum_out"] = accum
        P.add("act", (lambda o, i, f, kw: lambda e: e.activation(out=o, in_=i, func=f, **kw))(out, in_, func, kw),
              reads=reads, writes=writes)

    def tt(eng, out, in0, in1, op, reads, writes):
        P.add(eng, (lambda o, a, b, op: lambda e: e.tensor_tensor(o, a, b, op))(out, in0, in1, op), reads=reads, writes=writes)

    def ts(eng, out, in0, s1, s2, op0, op1, reads, writes):
        if op1 is None:
            P.add(eng, (lambda o, a, s1, op0: lambda e: e.tensor_scalar(o, a, s1, None, op0))(out, in0, s1, op0), reads=reads, writes=writes)
        else:
            P.add(eng, (lambda o, a, s1, s2, op0, op1: lambda e: e.tensor_scalar(o, a, s1, s2, op0, op1))(out, in0, s1, s2, op0, op1), reads=reads, writes=writes)

    def cp(eng, out, in_, reads, writes):
        if eng == "act":
            act(out, in_, AF.Copy, reads, writes)
        else:
            P.add(eng, (lambda o, i: lambda e: e.tensor_copy(o, i))(out, in_), reads=reads, writes=writes)

    def memset(eng, ap, val, writes):
        P.add(eng, (lambda a, v: lambda e: e.memset(a, v))(ap, val), writes=writes)

    def asel(out, in_, pattern, cmp_op, fill, base, cm, reads, writes):
        P.add("pool", (lambda o, i, p, c, f, b, m: lambda e: e.affine_select(o, i, pattern=p, compare_op=c, fill=f, base=b, channel_multiplier=m))(out, in_, pattern, cmp_op, fill, base, cm),
              reads=reads, writes=writes)

    psrot = [0]

    def next_ps(n=4):
        i = psrot[0] % n
        psrot[0] += 1
        return i

    memset("pool", identf[:], 1.0, ["identf"])
    asel(identf[:], identf[:], [[-1, 128]], ALU.is_equal, 0.0, 0, 1, ["identf"], ["identf"])
    cp("dve", identb[:], identf[:], ["identf"], ["identb"])
    memset("pool", Ltri[:], 1.0, ["Ltri"])
    asel(Ltri[:], Ltri[:], [[1, 128]], ALU.is_ge, 0.0, 0, -1, ["Ltri"], ["Ltri"])
    cp("dve", Ub[:], Ltri[:], ["Ltri"], ["Ub"])
    memset("pool", Elast[:], 1.0, ["Elast"])
    asel(Elast[:], Elast[:], [[0, 128]], ALU.is_equal, 0.0, -127, 1, ["Elast"], ["Elast"])
    memset("pool", Jf[:], 1.0, ["Jf"])
    asel(Jf[:], Jf[:], [[1, 128]], ALU.is_equal, 0.0, -127, 1, ["Jf"], ["Jf"])
    memset("pool", E31[:], 1.0, ["E31"])
    asel(E31[:], E31[:], [[0, 128]], ALU.is_equal, 0.0, -31, 1, ["E31"], ["E31"])
    memset("dve", onesb[:], 1.0, ["onesb"])
    memset("dve", SSQ[:], 0.0, ["SSQ"])

    dma("sp", GFIN[:], g_fin.partition_broadcast(128), [], ["GFIN"], "gfin")
    dma("sp", BFR[:], b_f.partition_broadcast(128), [], ["BFR"], "bfr")

    def load_featmajor(src_ap, n, dst_ap, key, sem):
        dma("sp", SMALL[0:n, :], src_ap, [], ["SMALL"], sem)
        tr(PS[6][:, 0:n], SMALL[0:n, :], identf[0:n, 0:n], ["SMALL", "identf"], ["PS6"])
        cp("dve", dst_ap, PS[6][:, 0:n], ["PS6"], [key])

    BADAT = sb("BADAT", [128, 96], F32)
    GMIXT = sb("GMIXT", [128, 16], F32)
    GFFNT = sb("GFFNT", [128, 16], F32)
    load_featmajor(b_ada, 96, BADAT[:], "BADAT", "small")
    load_featmajor(g_mix, 16, GMIXT[:], "GMIXT", "small")
    load_featmajor(g_ffn, 16, GFFNT[:], "GFFNT", "small")

    CS = XN[0:3, :]
    SCT = sb("SCT", [128, KC, 3], BF16)
    dma("sp", CS, c3, [], ["XN"], "cs")
    act(CS, CS, AF.Silu, ["XN"], ["XN"])
    for kc in range(KC):
        tr(PS[6][:, kc * 3:(kc + 1) * 3], XN[0:3, kc * 128:(kc + 1) * 128], identf[0:3, 0:3], ["XN", "identf"], ["PS6"])
    cp("dve", SCT[:], PS[6][:, 0:48].rearrange("p (k r) -> p k r", r=3), ["PS6"], ["SCT"])
    for cg in range(48):
        slot = cg % NSLOT
        wv = WS[slot][:].rearrange("p (k c) -> p k c", c=256)
        dma("pool", wv, w_ada[:, cg * 256:(cg + 1) * 256].rearrange("(k p) c -> p k c", p=128), [], ["WS%d" % slot], "ws%d" % slot)
        for oc in range(2):
            col = (cg * 2 + oc) * 3
            for kc in range(KC):
                mm(PS[5][:, col:col + 3], wv[:, kc, oc * 128:(oc + 1) * 128], SCT[:, kc, :], kc == 0, kc == KC - 1,
                   ["WS%d" % slot, "SCT"], ["PS5"])
    for r in range(3):
        tt("dve", ADAT[:, :, r], PS[5][:, 0:288].rearrange("p (v r) -> p v r", r=3)[:, :, r], BADAT[:], ALU.add, ["PS5", "BADAT"], ["ADAT"])
    for r in range(3):
        P.add("dve", (lambda r: lambda e: e.scalar_tensor_tensor(MODF[:, r, 0, :], ADAT[:, 16:32, r], 1.0, GMIXT[:], ALU.add, ALU.mult))(r),
              reads=["ADAT", "GMIXT"], writes=["MODF"])
        cp("dve", MODF[:, r, 1, :], ADAT[:, 0:16, r], ["ADAT"], ["MODF"])
        P.add("dve", (lambda r: lambda e: e.scalar_tensor_tensor(MODF[:, r, 2, :], ADAT[:, 64:80, r], 1.0, GFFNT[:], ALU.add, ALU.mult))(r),
              reads=["ADAT", "GFFNT"], writes=["MODF"])
        cp("dve", MODF[:, r, 3, :], ADAT[:, 48:64, r], ["ADAT"], ["MODF"])
    for r in range(3):
        tr(PS[6][0:96, 0:128], ADAT[:, :, r], identf[:], ["ADAT", "identf"], ["PS6"])
        cp("dve", SMALL[0:96, :], PS[6][0:96, 0:128], ["PS6"], ["SMALL"])
        dma("sp", adascr[r], SMALL[0:96, :], ["SMALL"], ["adascr"], "small")
    adaflat = adascr.rearrange("r v p -> r (v p)")

    def load_gates(row, nrows):
        dma("sp", GREP[0:nrows, 0, :], adaflat[row:row + 1, 2 * D:3 * D].partition_broadcast(nrows), ["adascr"], ["GREP"], "grep")
        dma("sp", GREP[0:nrows, 1, :], adaflat[row:row + 1, 5 * D:6 * D].partition_broadcast(nrows), ["adascr"], ["GREP"], "grep")

    T8 = XN[0:8, 0:768]
    dma("sp", T8[:, 0:256], relb[1:257, :].rearrange("w h -> h w"), [], ["XN"], "t8", allow_slow_non_contiguous=True)
    memset("dve", T8[:, 256:768], 0.0, ["XN"])
    ts("dve", T8[:, 256:768], T8[:, 256:768], T8[:, 255:256], 0.0, ALU.add, ALU.add, ["XN"], ["XN"])
    dma("sp", extscr, T8, ["XN"], ["extscr"], "t8")
    RT = XN[:, 1024:1664]
    for h in range(NH):
        src = bass.AP(tensor=extscr.tensor, offset=h * 768, ap=[[1, 128], [1, 640]])
        dma("sp", RT, src, ["extscr"], ["XN"], "rt")
        for half in range(2):
            mm(PS[6][:, 0:320], Jf[:], RT[:, half * 320:(half + 1) * 320], True, True, ["Jf", "XN"], ["PS6"])
            cp("dve", BH[:, h, half * 320:(half + 1) * 320], PS[6][:, 0:320], ["PS6"], ["BH"])
    memset("pool", BH[0:64, :, 576:640], NEGB, ["BH"])
    memset("pool", BH[64:128, :, 0:64], NEGB, ["BH"])

    def convert(name):
        srcname, c0, K, N, t = WSPEC[name]
        src = wsrc[srcname]
        scr = wscr[name]
        for kc in range(K // 128):
            rows = src[kc * 128:(kc + 1) * 128, c0:c0 + N]
            if t == "S":
                s = rows.rearrange("p (g c) -> p g c", c=256)
                d_ = scr[:, :, kc * 256:(kc + 1) * 256].rearrange("g p c -> p g c")
            elif t == "S8":
                s = rows.rearrange("p (g c) -> p g c", c=512)
                d_ = scr[:, :, kc * 512:(kc + 1) * 512].rearrange("g p c -> p g c")
            else:
                nkh = (K // 128 + 7) // 8
                s = rows.rearrange("p (g c) -> p g c", c=512)
                sv = scr.rearrange("(g h) p e -> g h p e", h=nkh)
                d_ = sv[:, kc // 8, :, (kc % 8) * 512:(kc % 8 + 1) * 512].rearrange("g p c -> p g c")
            dma("pool", d_, s, [], ["wscr_" + name], "cv_" + name)

    dma("pool", wf_scr.rearrange("p (k c) -> p k c", c=8),
        wsrc["w_in"][:, 3072:3080].rearrange("(k p) c -> p k c", p=128), [], ["wf_scr"], "cv_f")
    dma("sp", WF[:], wf_scr.rearrange("p (k c) -> p k c", c=8), ["wf_scr"], ["WF"], "wf")

    if WMODE == 0:
        for n in WORDER:
            convert(n)

    def w_src_view(name, ti):
        srcname, c0, K, N, t = WSPEC[name]
        src = wsrc[srcname]
        if t == "S":
            return src[:, c0 + ti * 256:c0 + (ti + 1) * 256].rearrange("(k p) c -> p k c", p=128), 16, 256
        if t == "S8":
            return src[:, c0 + ti * 512:c0 + (ti + 1) * 512].rearrange("(k p) c -> p k c", p=128), 8, 512
        nkh = (K // 128 + 7) // 8
        cg, kh = ti // nkh, ti % nkh
        r0 = kh * 1024
        r1 = min(K, r0 + 1024)
        return src[r0:r1, c0 + cg * 512:c0 + (cg + 1) * 512].rearrange("(k p) c -> p k c", p=128), (r1 - r0) // 128, 512

    def tile_schedule():
        sch = []
        for n in ("ka", "va", "kb", "vb"):
            for cg in range(2):
                for kh in range(2):
                    sch.append((n, cg * 2 + kh))
        for n in ("qa", "qb"):
            for cg in range(4):
                sch.append((n, cg))
        for cg2 in range(4):
            sch += [("za", cg2 * 2), ("za", cg2 * 2 + 1), ("oa", cg2), ("zb", cg2 * 2), ("zb", cg2 * 2 + 1), ("ob", cg2)]
        for cg in range(4):
            for kh in range(2):
                sch.append(("out", cg * 2 + kh))
        for cg in range(22):
            sch.append(("gate", cg))
            sch.append(("up", cg))
        for cg in range(4):
            for kh in range(6):
                sch.append(("down", cg * 6 + kh))
        return sch

    NTILES_ALL = SEQ // 512 + 1
    NSCHED = len(tile_schedule())
    full_sched = []
    for t in range(NTILES_ALL):
        full_sched += tile_schedule()
    wstate = {"next_load": 0, "next_use": 0}

    def w_issue():
        r = wstate["next_load"]
        if r >= len(full_sched):
            return
        name, ti = full_sched[r]
        slot = r % NSLOT
        skey = "wscr:%s:%d" % (name, ti)
        if r < NSCHED and WMODE == 1:
            v, nk, ncol = w_src_view(name, ti)
            dma("pool", WS[slot][:, 0:nk * ncol].rearrange("p (k c) -> p k c", c=ncol), v, [], ["WS%d" % slot], "ws%d" % slot)
            dma("sp", wscr[name][ti], WS[slot][:], ["WS%d" % slot], [skey], "wst%d" % slot)
        else:
            dma("sp", WS[slot][:], wscr[name][ti], [skey, "wscr_" + name], ["WS%d" % slot], "ws%d" % slot)
        wstate["next_load"] = r + 1

    def w_next(name, ti):
        r = wstate["next_use"]
        assert full_sched[r] == (name, ti), (full_sched[r], name, ti)
        wstate["next_use"] = r + 1
        slot = r % NSLOT
        return WS[slot], "WS%d" % slot

    def w_done():
        w_issue()

    for _ in range(NSLOT):
        w_issue()

    def emit_tile(ti, sample):
        if sample:
            nb, bs = 2, TS
            xsrc, ydst = xs, ys
            rows = [1, 2]
        else:
            nb, bs = 4, 128
            xsrc, ydst = xp[ti * 512:(ti + 1) * 512, :], yp[ti * 512:(ti + 1) * 512, :]
            rows = [0, 0, 0, 0]
        NT = nb * bs
        t0 = 0 if sample else ti * 512

        for b in range(nb):
            dma("sp", X[0:bs, b, :], xsrc[b * bs:(b + 1) * bs, :], [], ["X%d" % b], "x%d" % b)
        if not sample and ti == 0:
            load_gates(0, 128)

        def norm_to_hT(vi):
            P.add("dve", lambda e: e.memset(DUMMY[:, 0:1], 0.0), reads=[], writes=["hT", "DUMMY0"])
            def stats(b):
                memset("dve", SSQ[0:bs, b:b + 1], 0.0, ["SSQ%d" % b])
                act(SQJ[0:bs, :], X[0:bs, b, :], AF.Square, ["X%d" % b], ["SQJ", "SSQ%d" % b], accum=SSQ[0:bs, b:b + 1])
                ts("dve", RSTD[0:bs, b:b + 1], SSQ[0:bs, b:b + 1], 1.0 / D, EPS, ALU.mult, ALU.add, ["SSQ%d" % b], ["RSTD%d" % b])
                act(RSTD[0:bs, b:b + 1], RSTD[0:bs, b:b + 1], AF.Sqrt, ["RSTD%d" % b], ["RSTD%d" % b])
                P.add("dve", (lambda b: lambda e: e.reciprocal(RSTD[0:bs, b:b + 1], RSTD[0:bs, b:b + 1]))(b), reads=["RSTD%d" % b], writes=["RSTD%d" % b])
            stats(0)
            for b in range(nb):
                act(XN[0:bs, :], X[0:bs, b, :], AF.Copy, ["X%d" % b, "RSTD%d" % b], ["XN"], scale=RSTD[0:bs, b:b + 1])
                if b + 1 < nb:
                    stats(b + 1)
                r = rows[b]
                for c4 in range(4):
                    pi = next_ps(4)
                    for j in range(4):
                        c = c4 * 4 + j
                        tr(PS[pi][:, j * 128:j * 128 + bs], XN[0:bs, c * 128:(c + 1) * 128], identf[0:bs, 0:bs], ["XN", "identf"], ["PS%d" % pi])
                    for j in range(4):
                        c = c4 * 4 + j
                        if c4 % 2 == 0:
                            act(hT[:, c, b * bs:(b + 1) * bs], PS[pi][:, j * 128:j * 128 + bs], AF.Identity, ["PS%d" % pi, "MODF", "hT"], ["hTc%d" % c],
                                bias=MODF[:, r, vi * 2 + 1, c:c + 1], scale=MODF[:, r, vi * 2, c:c + 1])
                        else:
                            ts("dve", hT[:, c, b * bs:(b + 1) * bs], PS[pi][:, j * 128:j * 128 + bs],
                               MODF[:, r, vi * 2, c:c + 1], MODF[:, r, vi * 2 + 1, c:c + 1], ALU.mult, ALU.add, ["PS%d" % pi, "MODF", "hT"], ["hTc%d" % c])

        norm_to_hT(0)
        P.add("dve", lambda e: e.memset(DUMMY[:, 1:2], 0.0), reads=["hTc%d" % c for c in range(KC)], writes=["hT", "DUMMY1"])

        stg_rot = [0]
        fchain = []

        def tokmajor(name, outs):
            for cg in range(2):
                for kh in range(2):
                    wt, wk = w_next(name, cg * 2 + kh)
                    wv = wt[:].rearrange("p (k c) -> p k c", c=512)
                    for b in range(nb):
                        for k8 in range(8):
                            kc = kh * 8 + k8
                            mm(PS[b][0:bs, :], hT[:, kc, b * bs:(b + 1) * bs], wv[:, k8, :], kc == 0, kc == KC - 1,
                               ["hT", wk], ["PS%d" % b])
                    w_done()
                if fchain:
                    fchain.pop(0)()
                outl = []
                for b in range(nb):
                    si = stg_rot[0] % 2
                    stg_rot[0] += 1
                    sf = STGF[si]
                    cp("dve", sf[0:bs, :], PS[b][0:bs, :], ["PS%d" % b], ["STGF%d" % si])
                    sbf, sbk = next_stgb()
                    cp("act", sbf[0:bs, :], sf[0:bs, :], ["STGF%d" % si], [sbk])
                    if outs is not None and outs[1](b):
                        dma(STQ, outs[0](b)[:, cg * 512:(cg + 1) * 512], sf[0:bs, :], ["STGF%d" % si], [], "STGF%d" % si)
                    outl.append((cg, b, sbf, sbk))
                for item in outl:
                    yield item

        stgb_rot = [0]

        def next_stgb():
            i = stgb_rot[0] % 4
            stgb_rot[0] += 1
            return STGB[i], "STGB%d" % i

        ktb_rot = [0]

        def k_path(name, out_ap, kT_scr, out_rows_ok):
            for cg, b, sbf, sbk in tokmajor(name, (out_ap, out_rows_ok)):
                ki = ktb_rot[0] % 2
                ktb_rot[0] += 1
                ktb = KTB[ki]
                for j in range(4):
                    tr(PB[:, ki * 512 + j * 128:ki * 512 + j * 128 + bs], sbf[0:bs, j * 128:(j + 1) * 128], identb[0:bs, 0:bs], [sbk, "identb"], ["PB"])
                cp("dve", ktb[:, :, 0:bs], PB[:, ki * 512:(ki + 1) * 512].rearrange("p (j t) -> p j t", t=128)[:, :, 0:bs], ["PB"], ["KTB%d" % ki])
                dst = kT_scr[cg * 4:(cg + 1) * 4, :, t0 + b * bs:t0 + (b + 1) * bs].rearrange("h d t -> d h t")
                if sample:
                    dst = kT_scr[cg * 4:(cg + 1) * 4, :, b * 128:b * 128 + bs].rearrange("h d t -> d h t")
                gblk = b if sample else t0 // 128 + b
                dma(STQ, dst, ktb[:, :, 0:bs], ["KTB%d" % ki], ["%s:%d:%d" % (name, gblk, cg)], "KTB%d" % ki)

        def v_path(name, out_ap, v_scr, out_rows_ok):
            for cg, b, sbf, sbk in tokmajor(name, (out_ap, out_rows_ok)):
                if sample:
                    dst = v_scr[b * 128:b * 128 + bs, cg * 512:(cg + 1) * 512]
                else:
                    dst = v_scr[t0 + b * bs:t0 + (b + 1) * bs, cg * 512:(cg + 1) * 512]
                gblk = b if sample else t0 // 128 + b
                dma(STQ, dst, sbf[0:bs, :], [sbk], ["%s:%d:%d" % (name, gblk, cg)], sbk)

        if sample:
            fk_out = lambda b: fks[b * bs:(b + 1) * bs, :]
            fv_out = lambda b: fvs[b * bs:(b + 1) * bs, :]
            bk_out = lambda b: bks[b * bs:(b + 1) * bs, :]
            bv_out = lambda b: bvs[b * bs:(b + 1) * bs, :]
            always = lambda b: True
            band_ok = always
        else:
            fk_out = lambda b: fkp[t0 + b * bs:t0 + (b + 1) * bs, :]
            fv_out = lambda b: fvp[t0 + b * bs:t0 + (b + 1) * bs, :]
            bk_out = lambda b: bkp[b * bs:(b + 1) * bs, :]
            bv_out = lambda b: bvp[b * bs:(b + 1) * bs, :]
            always = lambda b: True
            band_ok = (lambda b: True) if ti == SEQ // 512 - 1 else (lambda b: False)

        k_path("ka", fk_out, kaT_scr, always)
        v_path("va", fv_out, va_scr, always)

        for b in range(nb):
            for kc in range(KC):
                mm(PS[6][0:bs, 0:8], hT[:, kc, b * bs:(b + 1) * bs], WF[:, kc, :], kc == 0, kc == KC - 1, ["hT", "WF"], ["PS6"])
            tt("dve", LTMP[0:bs, :], PS[6][0:bs, 0:8], BFR[0:bs, :], ALU.add, ["PS6", "BFR"], ["LTMP"])
            act(LTMP[0:bs, :], LTMP[0:bs, :], AF.Exp, ["LTMP"], ["LTMP"], scale=-1.0)
            ts("dve", LTMP[0:bs, :], LTMP[0:bs, :], 1.0, 0.0, ALU.add, ALU.add, ["LTMP"], ["LTMP"])
            act(LTMP[0:bs, :], LTMP[0:bs, :], AF.Ln, ["LTMP"], ["LTMP"])
            ts("dve", LFT[0:bs, b, :], LTMP[0:bs, :], -1.0, 0.0, ALU.mult, ALU.add, ["LTMP"], ["LFT"])
        if sample:
            for b in range(nb):
                dma(STQ, fls[b * bs:(b + 1) * bs, :], LFT[0:bs, b, :], ["LFT"], [], "lft")
        else:
            dma(STQ, flp[t0:t0 + 512, :].rearrange("(b p) h -> p b h", p=128), LFT[:, :, :], ["LFT"], [], "lft")
            for b in range(nb):
                def fstep(b=b):
                    gb = ti * 4 + b
                    mm(PS[6][:, 8:16], Ltri[:], LFT[:, b, :], True, gb == 0, ["Ltri", "LFT"], ["PS6"])
                    if gb > 0:
                        mm(PS[6][:, 8:16], Elast[:], Fall[:, gb - 1, :], False, True, ["Elast", "Fall"], ["PS6"])
                    cp("dve", Fall[:, gb, :], PS[6][:, 8:16], ["PS6"], ["Fall"])
                fchain.append(fstep)

        k_path("kb", bk_out, kbT_scr, band_ok)
        v_path("vb", bv_out, vb_scr, band_ok)
        while fchain:
            fchain.pop(0)()

        def featmajor(name, ncg, nk, ncol, rhs_of, evac):
            per = ncol // 128
            for cg in range(ncg):
                wt, wk = w_next(name, cg)
                wv = wt[:].rearrange("p (k c) -> p k c", c=ncol)
                for o in range(per):
                    pi = next_ps(4)
                    for k in range(nk):
                        rhs, rk = rhs_of(k)
                        mm(PS[pi][:, 0:NT], wv[:, k, o * 128:(o + 1) * 128], rhs, k == 0, k == nk - 1, [wk, rk], ["PS%d" % pi])
                    evac(cg * per + o, pi)
                w_done()

        h_rhs = lambda k: (hT[:, k, 0:NT], "hT")

        def q_evac(dst, key):
            def f(oc, pi):
                act(dst[:, oc, 0:NT], PS[pi][:, 0:NT], AF.Copy, ["PS%d" % pi], K_(key), scale=INV)
            return f
        featmajor("qa", 4, KC, 256, h_rhs, q_evac(QaT, "QaT"))
        featmajor("qb", 4, KC, 256, h_rhs, q_evac(QbT, "QbT"))

        pt_rot = [0]
        ks_rot = [0]

        def attn_head(h, QT, qkey, oT, okey, blocks, band):
            po = (h % 2) * 2
            pO, pD = PS[po], PS[po + 1]
            kO, kD = "PS%d" % po, "PS%d" % (po + 1)
            n = len(blocks)
            pend = []

            def issue_s(i):
                bl = blocks[i]
                pi = 4 + (i % 3)
                nk, c0, c1 = bl["nk"], bl["c0"], bl["c1"]
                kt, ktk = bl["kt"]
                mm(PS[pi][0:nk, c0:c1], kt, QT[:, h, c0:c1], True, not band, [ktk, qkey], ["PS%d" % pi])
                if band:
                    u0 = bl["u0"]
                    mm(PS[pi][0:nk, c0:c1], identb[0:nk, 0:nk], BH[0:nk, h, u0:u0 + (c1 - c0)], False, True, ["identb", "BH"], ["PS%d" % pi])
                return pi

            pis = {}
            pis[0] = issue_s(0)
            if n > 1:
                pis[1] = issue_s(1)
            for i in range(n):
                if i + 2 < n:
                    pis[i + 2] = issue_s(i + 2)
                bl = blocks[i]
                pi = pis[i]
                nk, c0, c1 = bl["nk"], bl["c0"], bl["c1"]
                pti = pt_rot[0] % 3
                pt_rot[0] += 1
                pt = PT[pti]
                ptk = "PT%d" % pti
                if bl.get("bias") is not None:
                    act(pt[0:nk, c0:c1], PS[pi][0:nk, c0:c1], AF.Exp, ["PS%d" % pi, "NB"], [ptk], bias=bl["bias"])
                else:
                    act(pt[0:nk, c0:c1], PS[pi][0:nk, c0:c1], AF.Exp, ["PS%d" % pi], [ptk])
                if bl.get("mask"):
                    w = bl["mask"]
                    tt("dve", pt[0:nk, c0:c0 + w], pt[0:nk, c0:c0 + w], Ub[0:nk, 0:w], ALU.mult, [ptk, "Ub"], [ptk])
                v, vk = bl["v"]
                mm(pO[:, c0:c1], v, pt[0:nk, c0:c1], i == 0, i == n - 1, [vk, ptk], [kO])
                mm(pD[:, c0:c1], onesb[0:nk, :], pt[0:nk, c0:c1], i == 0, i == n - 1, ["onesb", ptk], [kD])
            P.add("dve", (lambda pD: lambda e: e.reciprocal(RDEN[:, 0:NT], pD[:, 0:NT]))(pD), reads=[kD], writes=["RDEN"])
            tt("dve", oT[:, h, 0:NT], pO[:, 0:NT], RDEN[:, 0:NT], ALU.mult, [kO, "RDEN"], [okey])

        def stream_kv(h, kname, vname, kT_scr, v_scr, key0, nkeys):
            si = ks_rot[0] % 3
            ks_rot[0] += 1
            ks, vs = KS[si], VS[si]
            nblk = nkeys // 128
            kkeys = ["%s:%d:%d" % (kname, key0 // 128 + j, h // 4) for j in range(nblk)]
            vkeys = ["%s:%d:%d" % (vname, key0 // 128 + j, h // 4) for j in range(nblk)]
            dma("sp", ks[:, 0:nkeys], kT_scr[h, :, key0:key0 + nkeys], kkeys, K_("KS%d" % si), "ks%d" % si)
            dma("sp", vs[:, 0:nblk, :], v_scr[key0:key0 + nkeys, h * 128:(h + 1) * 128].rearrange("(b p) d -> p b d", p=128),
                vkeys, K_("VS%d" % si), "vs%d" % si)
            return ks, "KS%d" % si, vs, "VS%d" % si


        def sample_attention():
            CKs = [(KS[0], "KS0"), (KS[1], "KS1")]
            CVs = [(VS[i_].rearrange("p a b -> p (a b)"), "VS%d" % i_) for i_ in range(3)]
            KTSs = [(KS[2].rearrange("p (a b) -> p a b", a=8), "KS2")]
            rot = [0]
            for b in range(2):
                q0 = b * TS
                pO, pD = PS[b * 2], PS[b * 2 + 1]
                kO, kD = "PS%d" % (b * 2), "PS%d" % (b * 2 + 1)
                dma("sp", CL[:], cfl[b].rearrange("(k p) h -> p k h", p=128), [], ["CL"], "cl")
                for kb in range(32):
                    mm(PS[6][:, 8:16], Ltri[:], CL[:, kb, :], True, kb == 0, ["Ltri", "CL"], ["PS6"])
                    if kb > 0:
                        mm(PS[6][:, 8:16], Elast[:], Fall[:, kb - 1, :], False, True, ["Elast", "Fall"], ["PS6"])
                    cp("dve", Fall[:, kb, :], PS[6][:, 8:16], ["PS6"], ["Fall"])
                mm(PS[6][0:TS, 8:16], Ltri[0:TS, 0:TS], LFT[0:TS, b, :], True, False, ["Ltri", "LFT"], ["PS6"])
                mm(PS[6][0:TS, 8:16], Elast[:, 0:TS], Fall[:, 31, :], False, True, ["Elast", "Fall"], ["PS6"])
                cp("dve", Fall[0:TS, 32, :], PS[6][0:TS, 8:16], ["PS6"], ["Fall"])
                mm(PS[6][:, 16:24], E31[0:TS, :], Fall[0:TS, 32, :], True, True, ["E31", "Fall"], ["PS6"])
                cp("dve", Fref[:], PS[6][:, 16:24], ["PS6"], ["Fref"])
                for h in range(NH):
                    ts("dve", NB[:, 0:32, h], Fall[:, 0:32, h], -1.0, Fref[:, h:h + 1], ALU.mult, ALU.add, ["Fref", "Fall"], ["NB"])
                tt("dve", NB[0:TS, 32, :], Fref[0:TS, :], Fall[0:TS, 32, :], ALU.subtract, ["Fref", "Fall"], ["NB"])

                def run_attn(band, ncache, ck_src, cv_src, kname, vname, kT_scr, v_scr, QT, qkey, oT, okey):
                    nblk = ncache + 1

                    def stage1(kb):
                        i = rot[0]
                        rot[0] += 1
                        (ck, ckk), (cv, cvk), (kts, ktsk) = CKs[i % 2], CVs[i % 3], KTSs[0]
                        new = kb == ncache
                        nk = TS if new else 128
                        if not new:
                            dma("pool", ck[:, 0:1024], ck_src[b, kb * 128:(kb + 1) * 128, :], [], K_(ckk), "s" + ckk)
                            dma("pool", cv[:, 0:1024], cv_src[b, kb * 128:(kb + 1) * 128, :], [], K_(cvk), "s" + cvk)
                            for h in range(NH):
                                tr(PB[:, h * 128:(h + 1) * 128], ck[:, h * 128:(h + 1) * 128], identb[:], [ckk, "identb"], ["PB"])
                            cp("dve", kts[:, :, :], PB[:, :].rearrange("p (h t) -> p h t", h=8), ["PB"], K_(ktsk))
                        else:
                            kkeys = ["%s:%d:%d" % (kname, b, c) for c in range(2)]
                            vkeys = ["%s:%d:%d" % (vname, b, c) for c in range(2)]
                            dma("sp", kts[:, :, 0:TS], kT_scr[:, :, b * 128:b * 128 + TS].rearrange("h d t -> d h t"), kkeys, K_(ktsk), "s" + ktsk)
                            dma("sp", cv[0:TS, 0:1024], v_scr[b * 128:b * 128 + TS, :], vkeys, K_(cvk), "s" + cvk)
                        pi = 4 + (kb % 2)
                        for h in range(NH):
                            mm(PS[pi][0:nk, h * TS:(h + 1) * TS], kts[:, h, 0:nk], QT[:, h, q0:q0 + TS], True, not band, [ktsk, qkey], ["PS%d" % pi])
                            if band:
                                u0 = 0 if new else 512 - 128 * kb
                                mm(PS[pi][0:nk, h * TS:(h + 1) * TS], identb[0:nk, 0:nk], BH[0:nk, h, u0:u0 + TS], False, True, ["identb", "BH"], ["PS%d" % pi])
                        return (kb, new, nk, pi, cv, cvk)

                    def stage2(st):
                        kb, new, nk, pi, cv, cvk = st
                        pti = pt_rot[0] % 3
                        pt_rot[0] += 1
                        pt, ptk = PT[pti], "PT%d" % pti
                        hk = ["%s:h%d" % (ptk, h) for h in range(NH)]
                        if band:
                            act(pt[0:nk, 0:256], PS[pi][0:nk, 0:256], AF.Exp, ["PS%d" % pi, ptk], hk)
                        else:
                            for h in range(NH):
                                act(pt[0:nk, h * TS:(h + 1) * TS], PS[pi][0:nk, h * TS:(h + 1) * TS], AF.Exp, ["PS%d" % pi, "NB", ptk], [hk[h]],
                                    bias=NB[0:nk, kb, h:h + 1])
                            if new:
                                for h in range(NH):
                                    tt("dve", pt[0:nk, h * TS:(h + 1) * TS], pt[0:nk, h * TS:(h + 1) * TS], Ub[0:TS, 0:TS], ALU.mult, [hk[h], "Ub"], [hk[h]])
                        for h in range(NH):
                            mm(pO[:, h * TS:(h + 1) * TS], cv[0:nk, h * 128:(h + 1) * 128], pt[0:nk, h * TS:(h + 1) * TS], kb == 0, kb == nblk - 1, [cvk, hk[h]], [kO])
                        mm(pD[:, 0:256], onesb[0:nk, :], pt[0:nk, 0:256], kb == 0, kb == nblk - 1, ["onesb"] + hk, [kD])

                    st = stage1(0)
                    for kb in range(nblk):
                        nxt = stage1(kb + 1) if kb + 1 < nblk else None
                        stage2(st)
                        st = nxt
                    P.add("dve", (lambda pD: lambda e: e.reciprocal(RDEN[:, 0:256], pD[:, 0:256]))(pD), reads=[kD], writes=["RDEN"])
                    tt("dve", oT[:, :, q0:q0 + TS], pO[:, 0:256].rearrange("p (h t) -> p h t", h=8), RDEN[:, 0:256].rearrange("p (h t) -> p h t", h=8),
                       ALU.mult, [kO, "RDEN"], [okey])

                run_attn(False, 32, cfk, cfv, "ka", "va", kaT_scr, va_scr, QaT, "QaT", oaT, "oaT")
                run_attn(True, 4, cbk, cbv, "kb", "vb", kbT_scr, vb_scr, QbT, "QbT", obT, "obT")

        if not sample:
            mm(PS[6][:, 16:24], Elast[:], Fall[:, ti * 4 + 3, :], True, True, ["Elast", "Fall"], ["PS6"])
            cp("dve", Fref[:], PS[6][:, 16:24], ["PS6"], ["Fref"])
            nkb = ti * 4 + 4
            if nkb > NH:
                for h in range(NH):
                    ts("dve", NB[:, 0:nkb, h], Fall[:, 0:nkb, h], -1.0, Fref[:, h:h + 1], ALU.mult, ALU.add, ["Fref", "Fall"], ["NB"])
            else:
                for kb in range(nkb):
                    tt("dve", NB[:, kb, :], Fref[:], Fall[:, kb, :], ALU.subtract, ["Fref", "Fall"], ["NB"])
            for h in range(NH):
                blocks = []
                for c0k in range(0, nkb * 128, 1024):
                    nkeys = min(1024, nkb * 128 - c0k)
                    ks, ksk, vs, vsk = stream_kv(h, "ka", "va", kaT_scr, va_scr, c0k, nkeys)
                    for j in range(nkeys // 128):
                        kb = c0k // 128 + j
                        jb = kb - ti * 4
                        bl = dict(kt=(ks[:, j * 128:(j + 1) * 128], ksk), v=(vs[:, j, :], vsk), nk=128,
                                  c0=0, c1=512, bias=NB[:, kb, h:h + 1])
                        if jb >= 0:
                            bl["c0"] = jb * 128
                            bl["mask"] = 128
                        blocks.append(bl)
                attn_head(h, QaT, "QaT", oaT, "oaT", blocks, False)
            for h in range(NH):
                gk0 = max(0, ti * 4 - 4)
                nkeys = (ti * 4 + 4 - gk0) * 128
                ks, ksk, vs, vsk = stream_kv(h, "kb", "vb", kbT_scr, vb_scr, gk0 * 128, nkeys)
                blocks = []
                order = [4, 5, 6, 7, 3, 2, 1, 0]
                for m in order:
                    gkb = ti * 4 - 4 + m
                    if gkb < 0:
                        continue
                    j = gkb - gk0
                    dl = 8 - 2 * m
                    jlo = max(0, -dl)
                    jhi = min(7, 9 - dl)
                    c0, c1 = jlo * 64, (jhi + 1) * 64
                    blocks.append(dict(kt=(ks[:, j * 128:(j + 1) * 128], ksk), v=(vs[:, j, :], vsk), nk=128,
                                       c0=c0, c1=c1, u0=dl * 64 + c0))
                attn_head(h, QbT, "QbT", obT, "obT", blocks, True)
        else:
            sample_attention()

        for cg2 in range(4):
            for which, (zname, oname, oT_, okey) in enumerate((("za", "oa", oaT, "oaT"), ("zb", "ob", obT, "obT"))):
                zps = []
                for half in range(2):
                    wt, wk = w_next(zname, cg2 * 2 + half)
                    wv = wt[:].rearrange("p (k c) -> p k c", c=256)
                    for o in range(2):
                        pi = len(zps)
                        for k in range(KC):
                            mm(PS[pi][:, 0:NT], wv[:, k, o * 128:(o + 1) * 128], hT[:, k, 0:NT], k == 0, k == KC - 1, [wk, "hT"], ["PS%d" % pi])
                        act(SGS[pi][:, 0:NT], PS[pi][:, 0:NT], AF.Sigmoid, ["PS%d" % pi], ["SG%d" % pi])
                        zps.append(pi)
                    w_done()
                wt, wk = w_next(oname, cg2)
                wv = wt[:].rearrange("p (k c) -> p k c", c=512)
                for o in range(4):
                    oc = cg2 * 4 + o
                    pi = 4 + (o % 2)
                    for k in range(8):
                        mm(PS[pi][:, 0:NT], wv[:, k, o * 128:(o + 1) * 128], oT_[:, k, 0:NT], k == 0, k == 7, [wk, okey], ["PS%d" % pi])
                    if which == 0:
                        tt("dve", MAS[o][:, 0:NT], PS[pi][:, 0:NT], SGS[o][:, 0:NT], ALU.mult, ["PS%d" % pi, "SG%d" % o], ["MA%d" % o])
                    else:
                        tt("dve", MTMP[:, 0:NT], PS[pi][:, 0:NT], SGS[o][:, 0:NT], ALU.mult, ["PS%d" % pi, "SG%d" % o], ["MTMP"])
                        tt("dve", mT[:, oc, 0:NT], MTMP[:, 0:NT], MAS[o][:, 0:NT], ALU.add, ["MTMP", "MA%d" % o], K_("mT"))
                w_done()

        def resid_pass(name, nkh, lhs_of, gi):
            if sample:
                dma("sp", XN[0:bs, :], adaflat[2:3, (2 + 3 * gi) * D:(3 + 3 * gi) * D].partition_broadcast(bs), ["adascr"], ["XN"], "xng")
            for cg in range(4):
                for kh in range(nkh):
                    wt, wk = w_next(name, cg * nkh + kh)
                    wv = wt[:].rearrange("p (k c) -> p k c", c=512)
                    nk8 = 8 if (name != "down" or kh < 5) else 4
                    for b in range(nb):
                        for k8 in range(nk8):
                            kc = kh * 8 + k8
                            lhs, lk = lhs_of(kc, b)
                            last = (kh == nkh - 1 and k8 == nk8 - 1)
                            mm(PS[b][0:bs, :], lhs, wv[:, k8, :], kc == 0, last, K_(lk) + [wk], ["PS%d" % b])
                    w_done()
                for b in range(nb):
                    if sample and b == 1:
                        gsrc = XN[0:bs, cg * 512:(cg + 1) * 512]
                        gk = "XN"
                    else:
                        gsrc = GREP[0:bs, gi, cg * 512:(cg + 1) * 512]
                        gk = "GREP"
                    tt("dve", MTMP2[0:bs, :], PS[b][0:bs, :], gsrc, ALU.mult, ["PS%d" % b, gk], ["MTMP2"])
                    tt("dve", X[0:bs, b, cg * 512:(cg + 1) * 512], X[0:bs, b, cg * 512:(cg + 1) * 512], MTMP2[0:bs, :], ALU.add,
                       ["X%d" % b, "MTMP2"], ["X%d" % b])

        resid_pass("out", 2, lambda kc, b: (mT[:, kc, b * bs:(b + 1) * bs], "mT"), 0)

        norm_to_hT(1)
        P.add("dve", lambda e: e.memset(DUMMY[:, 1:2], 0.0), reads=["hTc%d" % c for c in range(KC)], writes=["hT", "DUMMY1"])
        for cg in range(22):
            wtg, wkg = w_next("gate", cg)
            wtu, wku = w_next("up", cg)
            wvg = wtg[:].rearrange("p (k c) -> p k c", c=256)
            wvu = wtu[:].rearrange("p (k c) -> p k c", c=256)
            for o in range(2):
                fc = cg * 2 + o
                pg = next_ps(4)
                for k in range(KC):
                    mm(PS[pg][:, 0:NT], wvg[:, k, o * 128:(o + 1) * 128], hT[:, k, 0:NT], k == 0, k == KC - 1, [wkg, "hT"], ["PS%d" % pg])
                pu = next_ps(4)
                for k in range(KC):
                    mm(PS[pu][:, 0:NT], wvu[:, k, o * 128:(o + 1) * 128], hT[:, k, 0:NT], k == 0, k == KC - 1, [wku, "hT"], ["PS%d" % pu])
                sgi = fc % 4
                act(SGS[sgi][:, 0:NT], PS[pg][:, 0:NT], AF.Silu, ["PS%d" % pg], ["SG%d" % sgi])
                tt("dve", actT[:, fc, 0:NT], PS[pu][:, 0:NT], SGS[sgi][:, 0:NT], ALU.mult, ["PS%d" % pu, "SG%d" % sgi], K_("actT"))
            w_done()
            w_done()
        resid_pass("down", 6, lambda kc, b: (actT[:, kc, b * bs:(b + 1) * bs], "actT"), 1)

        for b in range(nb):
            memset("dve", SSQ[0:bs, b:b + 1], 0.0, ["SSQ%d" % b])
            act(SQJ[0:bs, :], X[0:bs, b, :], AF.Square, ["X%d" % b], ["SQJ", "SSQ%d" % b], accum=SSQ[0:bs, b:b + 1])
            ts("dve", RSTD[0:bs, b:b + 1], SSQ[0:bs, b:b + 1], 1.0 / D, EPS, ALU.mult, ALU.add, ["SSQ%d" % b], ["RSTD%d" % b])
            act(RSTD[0:bs, b:b + 1], RSTD[0:bs, b:b + 1], AF.Sqrt, ["RSTD%d" % b], ["RSTD%d" % b])
            P.add("dve", (lambda b: lambda e: e.reciprocal(RSTD[0:bs, b:b + 1], RSTD[0:bs, b:b + 1]))(b), reads=["RSTD%d" % b], writes=["RSTD%d" % b])
            P.add("dve", (lambda b: lambda e: e.scalar_tensor_tensor(X[0:bs, b, :], X[0:bs, b, :], RSTD[0:bs, b:b + 1], GFIN[0:bs, :], ALU.mult, ALU.mult))(b),
                  reads=["X%d" % b, "RSTD%d" % b, "GFIN"], writes=["X%d" % b])
            dma(STQ, ydst[b * bs:(b + 1) * bs, :], X[0:bs, b, :], ["X%d" % b], [], "x%d" % b)


    for ti in range(ntiles):
        emit_tile(ti, False)
    if do_sample:
        load_gates(1, TS)
        emit_tile(SEQ // 512, True)

    P.emit()
    return nc


_NC_CACHE = {}


def kernel(x_prompt, x_sample, cache_fox_k, cache_fox_v, cache_fox_logf, cache_band_k, cache_band_v,
           c_prompt, c_sample, w_ada, b_ada, g_mix, w_in, b_f, rel_bias, w_oa, w_ob, w_out,
           g_ffn, w_gate, w_up, w_down, g_final):
    f = lambda a: np.ascontiguousarray(np.asarray(a, dtype=np.float32))
    if "nc" not in _NC_CACHE:
        _NC_CACHE["nc"] = build_program()
    nc = _NC_CACHE["nc"]
    shared = {
        "w_ada": f(w_ada)[0], "b_ada": f(b_ada)[0].reshape(96, 128), "g_mix": f(g_mix)[0].reshape(16, 128),
        "g_ffn": f(g_ffn)[0].reshape(16, 128), "g_fin": f(g_final).reshape(1, D), "b_f": f(b_f)[0].reshape(1, NH),
        "relb": f(rel_bias)[0], "w_in": f(w_in)[0], "w_oa": f(w_oa)[0], "w_ob": f(w_ob)[0], "w_out": f(w_out)[0],
        "w_gate": f(w_gate)[0], "w_up": f(w_up)[0], "w_down": f(w_down)[0],
    }
    xpf, xsf = f(x_prompt), f(x_sample)
    cfkf, cfvf, cflf = f(cache_fox_k)[0], f(cache_fox_v)[0], f(cache_fox_logf)[0]
    cbkf, cbvf = f(cache_band_k)[0], f(cache_band_v)[0]
    cpf, csf = f(c_prompt), f(c_sample)
    in_maps = []
    for i in range(NCORES):
        m = dict(shared)
        m["xp"] = xpf[i]
        m["xs"] = xsf[2 * i:2 * i + 2].reshape(2 * TS, D)
        m["cfk"] = cfkf[2 * i:2 * i + 2].reshape(2, PAST, NH * HD)
        m["cfv"] = cfvf[2 * i:2 * i + 2].reshape(2, PAST, NH * HD)
        m["cfl"] = cflf[2 * i:2 * i + 2]
        m["cbk"] = cbkf[2 * i:2 * i + 2].reshape(2, LB, NH * HD)
        m["cbv"] = cbvf[2 * i:2 * i + 2].reshape(2, LB, NH * HD)
        m["c3"] = np.ascontiguousarray(np.concatenate([cpf[i:i + 1], csf[2 * i:2 * i + 2]], axis=0))
        in_maps.append(m)
    res = run_bass_kernel_spmd(nc, in_maps, core_ids=list(range(NCORES)))
    R = res.results
    cat = lambda k: np.stack([np.asarray(r[k]) for r in R], axis=0)
    y_prompt = cat("yp").reshape(8, SEQ, D)
    y_sample = cat("ys").reshape(16, TS, D)
    fk_p = cat("fkp").reshape(1, 8, SEQ, NH, HD)
    fv_p = cat("fvp").reshape(1, 8, SEQ, NH, HD)
    fl_p = cat("flp").reshape(1, 8, SEQ, NH)
    bk_p = cat("bkp").reshape(1, 8, LB, NH, HD)
    bv_p = cat("bvp").reshape(1, 8, LB, NH, HD)
    fk_s = cat("fks").reshape(1, 16, TS, NH, HD)
    fv_s = cat("fvs").reshape(1, 16, TS, NH, HD)
    fl_s = cat("fls").reshape(1, 16, TS, NH)
    bk_s = cat("bks").reshape(1, 16, TS, NH, HD)
    bv_s = cat("bvs").reshape(1, 16, TS, NH, HD)
    return (y_prompt, y_sample, fk_p, fv_p, fl_p, bk_p, bv_p, fk_s, fv_s, fl_s, bk_s, bv_s)
```
